# Optimizing a Trainium2 kernel written in Bass

```python
import jax, jax.numpy as jnp
from jax import lax
import numpy as np

D_MODEL = 1024
BATCH = 4
SEQ = 4096
DEPTH = 1

CHUNK = 64
LEFT_CHUNKS = 8
BAND = (LEFT_CHUNKS + 1) * CHUNK
ATT_HEADS = 8
ATT_HEAD_DIM = 64
ATT_WIDTH = ATT_HEADS * ATT_HEAD_DIM
MAX_REL = 128
LRU_WIDTH = D_MODEL
LRU_BLOCKS = 16
LRU_BLOCK = LRU_WIDTH // LRU_BLOCKS
CONV_WIDTH = 4
LRU_C = 8.0
D_FF = 2816
N_SUB = 3
EPS = 1e-6
PROJ_SIZES = (ATT_WIDTH, ATT_WIDTH, ATT_WIDTH, LRU_WIDTH, LRU_WIDTH, D_MODEL, D_MODEL)
PROJ_WIDTH = sum(PROJ_SIZES)

kernel_name = "hybrid_chunked_attn_rglru_macaron_block"


def rmsnorm(x, g):
    xf = x.astype(jnp.float32)
    y = xf * lax.rsqrt(jnp.mean(xf * xf, axis=-1, keepdims=True) + EPS)
    return (y * g.astype(jnp.float32)).astype(x.dtype)


def modulate(h, shift, scale):
    return h * (1 + scale[:, None, :]) + shift[:, None, :]


def swiglu(h, w_gu, w_down):
    g, u = jnp.split(h @ w_gu, 2, axis=-1)
    return (jax.nn.silu(g) * u) @ w_down


def chunked_attention(q, k, v, rel_bias):
    b, s, _ = q.shape
    nc = s // CHUNK
    q = q.reshape(b, nc, CHUNK, ATT_HEADS, ATT_HEAD_DIM)
    k = k.reshape(b, nc, CHUNK, ATT_HEADS, ATT_HEAD_DIM)
    v = v.reshape(b, nc, CHUNK, ATT_HEADS, ATT_HEAD_DIM)
    pad = ((0, 0), (LEFT_CHUNKS, 0), (0, 0), (0, 0), (0, 0))
    kp, vp = jnp.pad(k, pad), jnp.pad(v, pad)
    kb = jnp.concatenate([kp[:, j:j + nc] for j in range(LEFT_CHUNKS + 1)], axis=2)
    vb = jnp.concatenate([vp[:, j:j + nc] for j in range(LEFT_CHUNKS + 1)], axis=2)
    scores = jnp.einsum('bnqhd,bnkhd->bhnqk', q, kb).astype(jnp.float32) * (ATT_HEAD_DIM ** -0.5)
    qi = jnp.arange(CHUNK)[:, None]
    kj = jnp.arange(BAND)[None, :]
    rel = jnp.clip(qi - kj + LEFT_CHUNKS * CHUNK, -MAX_REL, MAX_REL) + MAX_REL
    bias = rel_bias.astype(jnp.float32)[:, rel]
    scores = scores + bias[None, :, None, :, :]
    valid = (jnp.arange(nc)[:, None] - LEFT_CHUNKS + jnp.arange(BAND)[None, :] // CHUNK) >= 0
    scores = jnp.where(valid[None, None, :, None, :], scores, jnp.finfo(jnp.float32).min)
    p = jax.nn.softmax(scores, axis=-1).astype(v.dtype)
    out = jnp.einsum('bhnqk,bnkhd->bnqhd', p, vb)
    return out.reshape(b, s, ATT_WIDTH)


def causal_depthwise_conv(x, w, bias):
    rhs = w[:, None, :]
    y = lax.conv_general_dilated(x, rhs, window_strides=(1,), padding=[(CONV_WIDTH - 1, 0)],
                                 dimension_numbers=('NWC', 'WIO', 'NWC'),
                                 feature_group_count=x.shape[-1])
    return y + bias


def block_diag_linear(x, w, bias):
    b, s, _ = x.shape
    xb = x.reshape(b, s, LRU_BLOCKS, LRU_BLOCK)
    return jnp.einsum('bsnk,nkj->bsnj', xb, w).reshape(b, s, LRU_WIDTH) + bias


def rg_lru(x, w_a, b_a, w_x, b_x, lam):
    r = jax.nn.sigmoid(block_diag_linear(x, w_a, b_a).astype(jnp.float32))
    i = jax.nn.sigmoid(block_diag_linear(x, w_x, b_x).astype(jnp.float32))
    log_a = -LRU_C * r * jax.nn.softplus(-lam.astype(jnp.float32))
    a = jnp.exp(log_a)
    mult = jnp.sqrt(-jnp.expm1(2.0 * log_a))
    u = mult * (i * x.astype(jnp.float32))

    def combine(left, right):
        a1, b1 = left
        a2, b2 = right
        return a1 * a2, a2 * b1 + b2

    _, h = lax.associative_scan(combine, (a, u), axis=1)
    return h.astype(x.dtype)


def mixer(h, w_in, rel_bias, conv_w, conv_b, lru_wa, lru_ba, lru_wx, lru_bx, lru_lambda,
          w_att_o, w_rec_o, w_out):
    proj = h @ w_in
    idx = [int(v) for v in np.cumsum(PROJ_SIZES)[:-1]]
    q, k, v, xr, yr, g_att, g_rec = jnp.split(proj, idx, axis=-1)
    att = chunked_attention(q, k, v, rel_bias) @ w_att_o
    xr = causal_depthwise_conv(xr, conv_w, conv_b)
    rec = (rg_lru(xr, lru_wa, lru_ba, lru_wx, lru_bx, lru_lambda) * jax.nn.gelu(yr)) @ w_rec_o
    merged = jax.nn.sigmoid(g_att) * att + jax.nn.sigmoid(g_rec) * rec
    return merged @ w_out


def sandwich(x, fn, g_pre, g_post, shift, scale, gate, res_w):
    h = modulate(rmsnorm(x, g_pre), shift, scale)
    y = rmsnorm(fn(h), g_post)
    return x + res_w * gate[:, None, :] * y


def setup_inputs(seed: int = 0) -> dict:
    key = jax.random.key(seed)
    ks = jax.random.split(key, 24)
    L, D, F, W = DEPTH, D_MODEL, D_FF, LRU_WIDTH
    nrm = lambda k, shape, fan_in: jax.random.normal(k, shape, jnp.float32) * (fan_in ** -0.5)
    u = jax.random.uniform(ks[20], (L, W), jnp.float32, 0.9, 0.999)
    p = u ** (1.0 / LRU_C)
    lam = jnp.log(p) - jnp.log1p(-p)
    return {
        "x": jax.random.normal(ks[0], (BATCH, SEQ, D), jnp.float32),
        "c": jax.random.normal(ks[1], (BATCH, D), jnp.float32),
        "w_ada": nrm(ks[2], (L, D, N_SUB * 3 * D), D) * 0.5,
        "b_ada": 0.02 * jax.random.normal(ks[3], (L, N_SUB * 3 * D), jnp.float32),
        "norm_pre": 1.0 + 0.05 * jax.random.normal(ks[4], (L, N_SUB, D), jnp.float32),
        "norm_post": 1.0 + 0.05 * jax.random.normal(ks[5], (L, N_SUB, D), jnp.float32),
        "ffn1_w_gu": nrm(ks[6], (L, D, 2 * F), D),
        "ffn1_w_down": nrm(ks[7], (L, F, D), F),
        "w_in": nrm(ks[8], (L, D, PROJ_WIDTH), D),
        "rel_bias": 0.5 * jax.random.normal(ks[9], (L, ATT_HEADS, 2 * MAX_REL + 1), jnp.float32),
        "conv_w": nrm(ks[10], (L, CONV_WIDTH, W), CONV_WIDTH),
        "conv_b": 0.02 * jax.random.normal(ks[11], (L, W), jnp.float32),
        "lru_wa": nrm(ks[12], (L, LRU_BLOCKS, LRU_BLOCK, LRU_BLOCK), LRU_BLOCK),
        "lru_ba": 0.02 * jax.random.normal(ks[13], (L, W), jnp.float32),
        "lru_wx": nrm(ks[14], (L, LRU_BLOCKS, LRU_BLOCK, LRU_BLOCK), LRU_BLOCK),
        "lru_bx": 0.02 * jax.random.normal(ks[15], (L, W), jnp.float32),
        "lru_lambda": lam,
        "w_att_o": nrm(ks[16], (L, ATT_WIDTH, D), ATT_WIDTH),
        "w_rec_o": nrm(ks[17], (L, W, D), W),
        "w_out": nrm(ks[18], (L, D, D), D),
        "ffn2_w_gu": nrm(ks[19], (L, D, 2 * F), D),
        "ffn2_w_down": nrm(ks[21], (L, F, D), F),
    }


def reference(x, c, w_ada, b_ada, norm_pre, norm_post, ffn1_w_gu, ffn1_w_down, w_in, rel_bias,
              conv_w, conv_b, lru_wa, lru_ba, lru_wx, lru_bx, lru_lambda, w_att_o, w_rec_o,
              w_out, ffn2_w_gu, ffn2_w_down):
    b = x.shape[0]
    c_act = jax.nn.silu(c)
    for l in range(DEPTH):
        mod = (c_act @ w_ada[l] + b_ada[l]).reshape(b, N_SUB, 3, D_MODEL)
        ffn1 = lambda h: swiglu(h, ffn1_w_gu[l], ffn1_w_down[l])
        mix = lambda h: mixer(h, w_in[l], rel_bias[l], conv_w[l], conv_b[l], lru_wa[l], lru_ba[l],
                              lru_wx[l], lru_bx[l], lru_lambda[l], w_att_o[l], w_rec_o[l], w_out[l])
        ffn2 = lambda h: swiglu(h, ffn2_w_gu[l], ffn2_w_down[l])
        x = sandwich(x, ffn1, norm_pre[l, 0], norm_post[l, 0], mod[:, 0, 0], mod[:, 0, 1], mod[:, 0, 2], 0.5)
        x = sandwich(x, mix, norm_pre[l, 1], norm_post[l, 1], mod[:, 1, 0], mod[:, 1, 1], mod[:, 1, 2], 1.0)
        x = sandwich(x, ffn2, norm_pre[l, 2], norm_post[l, 2], mod[:, 2, 0], mod[:, 2, 1], mod[:, 2, 2], 0.5)
    return x
```

```python
import numpy as np
import concourse.bass as bass
import concourse.mybir as mybir
from concourse.bass_utils import run_bass_kernel_spmd

F32 = mybir.dt.float32
BF16 = mybir.dt.bfloat16
AF = mybir.ActivationFunctionType
ALU = mybir.AluOpType

NCORES = 8
D = 1024
DC = 8
T = 2048
TT = 512
NT = 4
FF = 2816
FC = 22
EPS = 1e-6
NEG = -30000.0

SB_BASE = 16512
SB_TOP = 229344


class Emitter:
    ENGS = ("pe", "act", "dve", "pool", "sp")

    def __init__(self, nc):
        self.nc = nc
        self.streams = {e: [] for e in self.ENGS}
        self.cnt = {e: 0 for e in self.ENGS}
        self.known = {e: {} for e in self.ENGS}
        self.last_write = {}
        self.readers = {}
        self.dma_cnt = {}
        self.sem_names = set(self.ENGS)

    def _need(self, eng, ticket):
        if ticket is None:
            return
        src, val = ticket
        if src == eng and eng in ("pe", "sp"):
            return
        if self.known[eng].get(src, 0) >= val:
            return
        self.known[eng][src] = val
        self.streams[eng].append(("wait", src, val))

    def _deps(self, eng, reads, writes):
        for r in reads:
            self._need(eng, self.last_write.get(r))
        for w in writes:
            self._need(eng, self.last_write.get(w))
            for t in self.readers.get(w, ()):
                self._need(eng, t)

    def _commit(self, ticket, reads, writes):
        for r in reads:
            self.readers.setdefault(r, []).append(ticket)
        for w in writes:
            self.last_write[w] = ticket
            self.readers[w] = []

    def op(self, eng, fn, reads=(), writes=(), signal=True):
        self._deps(eng, reads, writes)
        if signal:
            self.cnt[eng] += 1
            ticket = (eng, self.cnt[eng])
        else:
            ticket = (eng, self.cnt[eng] + 1)
        self.streams[eng].append(("inst", fn, signal))
        self._commit(ticket, reads, writes)
        return ticket

    def dma(self, q, sem, fn, reads=(), writes=()):
        self.sem_names.add(sem)
        self._deps(q, reads, writes)
        self.dma_cnt[sem] = self.dma_cnt.get(sem, 0) + 16
        ticket = (sem, self.dma_cnt[sem])
        self.streams[q].append(("dma", fn, sem, 16))
        self._commit(ticket, reads, writes)
        return ticket

    def cc(self, sem, fn, reads=(), writes=()):
        self.sem_names.add(sem)
        self._deps("pool", reads, writes)
        self.dma_cnt[sem] = self.dma_cnt.get(sem, 0) + 1
        ticket = (sem, self.dma_cnt[sem])
        self.streams["pool"].append(("dma", fn, sem, 1))
        self._commit(ticket, reads, writes)
        return ticket

    def retarget(self, keys, sem):
        t = (sem, self.dma_cnt[sem])
        for k in keys:
            self.last_write[k] = t

    def barrier(self):
        for e in self.ENGS:
            for s in ("pe", "act", "dve", "pool"):
                if s != e and self.cnt[s] > 0:
                    self._need(e, (s, self.cnt[s]))
            if e in ("pe", "act", "dve", "pool") and self.cnt[e] > 0 and e != "pe":
                self._need(e, (e, self.cnt[e]))
            for s, v in self.dma_cnt.items():
                self._need(e, (s, v))

    def final_wait(self, eng, sems):
        for s in sems:
            self._need(eng, (s, self.dma_cnt[s]))

    def replay(self, eng, handle, sems):
        for item in self.streams[eng]:
            if item[0] == "wait":
                handle.wait_ge(sems[item[1]], item[2])
            elif item[0] == "inst":
                ins = item[1](handle)
                if item[2]:
                    ins.then_inc(sems[eng], 1)
            else:
                ins = item[1](handle)
                ins.then_inc(sems[item[2]], item[3])


def build_program():
    nc = bass.Bass("TRN2", target_bir_lowering=False)
    E = Emitter(nc)

    def din(name, shape, dt=F32):
        return nc.dram_tensor(name, list(shape), dt, kind="ExternalInput").ap()

    x_in = din("x_in", [T, D])
    out_d = nc.dram_tensor("out", [T, D], F32, kind="ExternalOutput").ap()
    c_pc = din("c_pc", [128, DC])
    w_ada = din("w_ada_r", [128, 36, 1024])
    b_ada = din("b_ada_r", [128, 72])
    npre = din("npre_r", [128, 3, DC])
    npost = din("npost_r", [128, 3, DC])
    wgu_d = [din("wgu1_r", [FC, 128, 8, 256]), din("wgu2_r", [FC, 128, 8, 256])]
    wdn_d = [din("wdn1_r", [DC, 128, FC, 128]), din("wdn2_r", [DC, 128, FC, 128])]
    win_d = din("win_r", [44, 128, 8, 128])
    watt_d = din("watt_r", [DC, 128, 4, 128])
    wrec_d = din("wrec_r", [DC, 128, 8, 128])
    wout_d = din("wout_r", [DC, 128, 8, 128])
    convw_d = din("convw_r", [128, DC, 4])
    convb_d = din("convb_r", [128, DC])
    lba_d = din("lba_r", [128, DC])
    lbx_d = din("lbx_r", [128, DC])
    lam_d = din("lam_r", [128, DC])
    lwa_d = din("lwa_r", [128, DC, 64])
    lwx_d = din("lwx_r", [128, DC, 64])
    biasg_d = din("biasg_r", [128, 8, 384])
    biasc_d = din("biasc_r", [128, 8])
    ident_d = din("ident", [128, 128])
    flag_d = din("flag", [128, 1])

    ccA_in = nc.dram_tensor("ccA_in", [128, 32], F32, kind="Internal").ap()
    ccA_out = nc.dram_tensor("ccA_out", [256, 32], F32, kind="Internal").ap()
    ccB_in = nc.dram_tensor("ccB_in", [128, 8], F32, kind="Internal").ap()
    ccB_out = nc.dram_tensor("ccB_out", [256, 8], F32, kind="Internal").ap()
    ccC_in = nc.dram_tensor("ccC_in", [128, 4096], BF16, kind="Internal").ap()
    ccC_out = nc.dram_tensor("ccC_out", [256, 4096], BF16, kind="Internal").ap()
    ccD_in = nc.dram_tensor("ccD_in", [128, 36], F32, kind="Internal").ap()
    ccD_out = nc.dram_tensor("ccD_out", [256, 36], F32, kind="Internal").ap()
    PAIRS = [[0, 1], [2, 3], [4, 5], [6, 7]]

    def sb(name, shape, dt, off):
        nbytes = int(np.prod(shape[1:])) * (4 if dt == F32 else 2)
        assert off % 32 == 0 and off >= SB_BASE and off + nbytes <= SB_TOP, (name, off, nbytes)
        return nc.alloc_sbuf_tensor_at(name, list(shape), dt, offset=off)

    o = SB_BASE
    xT = sb("xT", [128, DC, T], F32, o); o += DC * T * 4
    CB = o
    def cst(name, shape, dt):
        nonlocal o
        t = sb(name, shape, dt, o)
        nb = int(np.prod(shape[1:])) * (4 if dt == F32 else 2)
        o += (nb + 31) // 32 * 32
        return t
    ident = cst("ident", [128, 128], F32)
    ones_bf = cst("ones_bf", [128, 128], BF16)
    flagones = cst("flagones", [128, 64], BF16)
    zeros5 = cst("zeros5", [128, TT], BF16)
    mod = cst("mod", [128, 72], F32)
    badas = cst("badas", [128, 72], F32)
    npre_s = cst("npre_s", [128, 3, DC], F32)
    npost_s = cst("npost_s", [128, 3, DC], F32)
    gs32 = cst("gs32", [128, 3, DC], F32)
    gg32 = cst("gg32", [128, 3, DC], F32)
    cpc_s = cst("cpc_s", [128, DC], F32)
    cact = cst("cact", [128, DC], F32)
    cact_bf = cst("cact_bf", [128, DC], BF16)
    convw_s = cst("convw_s", [128, DC, 4], F32)
    convb_s = cst("convb_s", [128, DC], F32)
    lba_s = cst("lba_s", [128, DC], F32)
    lbx_s = cst("lbx_s", [128, DC], F32)
    lam_s = cst("lam_s", [128, DC], F32)
    nsp = cst("nsp", [128, DC], F32)
    hnsp = cst("hnsp", [128, DC], F32)
    hba = cst("hba", [128, DC], F32)
    hbx = cst("hbx", [128, DC], F32)
    qrt = cst("qrt", [128, 1], F32)
    flag_s = cst("flag_s", [128, 1], F32)
    epsb = cst("epsb", [128, 1], F32)
    biasc_s = cst("biasc_s", [128, 8], F32)
    hend = cst("hend", [128, DC], F32)
    hin = cst("hin", [128, DC], F32)
    hin_raw = cst("hin_raw", [128, DC], F32)
    xrt = cst("xrt", [128, DC, 4], F32)
    xrh_raw = cst("xrh_raw", [128, DC, 4], F32)
    xrh = cst("xrh", [128, DC, 4], F32)
    PB = (o + 31) // 32 * 32
    PSZ = SB_TOP - PB

    psum = nc.alloc_psum_tensor("psum", [128, 4096], F32)

    def bank(b, w=512, off=0):
        return psum[:, b * 512 + off: b * 512 + off + w]

    sems = {}

    consts = []

    def cload(dst, src, key):
        E.dma("sp", "const", lambda e, d=dst, s=src: e.dma_start(out=d, in_=s), writes=[key])
        consts.append(key)

    cload(ident[:], ident_d[:, :], "ident")
    cload(badas[:], b_ada[:, :], "badas")
    cload(npre_s[:], npre[:, :, :], "npre")
    cload(npost_s[:], npost[:, :, :], "npost")
    cload(cpc_s[:], c_pc[:, :], "cpc")
    cload(convw_s[:], convw_d[:, :, :], "convw")
    cload(convb_s[:], convb_d[:, :], "convb")
    cload(lba_s[:], lba_d[:, :], "lba")
    cload(lbx_s[:], lbx_d[:, :], "lbx")
    cload(lam_s[:], lam_d[:, :], "lam")
    cload(flag_s[:], flag_d[:, :], "flag")
    cload(biasc_s[:], biasc_d[:, :], "biasc")
    E.retarget(consts, "const")

    E.op("dve", lambda e: e.memset(ones_bf[:], 1.0), writes=["ones"])
    E.op("dve", lambda e: e.memset(zeros5[:], 0.0), writes=["zeros5"])
    E.op("dve", lambda e: e.memset(epsb[:], float(D * EPS)), writes=["epsb"])
    E.op("dve", lambda e: e.tensor_scalar(out=flagones[:], in0=ones_bf[:, 0:64], scalar1=flag_s[:, 0:1],
                                           scalar2=None, op0=ALU.mult), reads=["ones", "flag"], writes=["flagones"])
    E.op("act", lambda e: e.activation(out=cact[:], in_=cpc_s[:], func=AF.Silu), reads=["cpc"], writes=["cact"])
    E.op("act", lambda e: e.copy(out=cact_bf[:], in_=cact[:]), reads=["cact"], writes=["cactbf"])
    E.op("act", lambda e: e.activation(out=nsp[:], in_=lam_s[:], func=AF.Exp, scale=-1.0), reads=["lam"], writes=["nsp"])
    E.op("act", lambda e: e.activation(out=nsp[:], in_=nsp[:], func=AF.Ln, bias=1.0), reads=["nsp"], writes=["nsp"])
    E.op("dve", lambda e: e.tensor_scalar(out=hnsp[:], in0=nsp[:], scalar1=-4.0, scalar2=None, op0=ALU.mult),
         reads=["nsp"], writes=["hnsp"])
    E.op("dve", lambda e: e.tensor_scalar(out=nsp[:], in0=nsp[:], scalar1=-8.0, scalar2=None, op0=ALU.mult),
         reads=["nsp"], writes=["nsp"])
    E.op("dve", lambda e: e.tensor_scalar(out=hba[:], in0=lba_s[:], scalar1=0.5, scalar2=None, op0=ALU.mult), reads=["lba"], writes=["hba"])
    E.op("dve", lambda e: e.tensor_scalar(out=hbx[:], in0=lbx_s[:], scalar1=0.5, scalar2=None, op0=ALU.mult), reads=["lbx"], writes=["hbx"])
    E.op("dve", lambda e: e.memset(qrt[:], 0.25), writes=["qrt"])

    adab = [sb("adab%d" % i, [128, 9, 1024], BF16, PB + 16384 + i * 18432) for i in range(2)]
    modp = sb("modp", [128, 36], F32, PB + 16384 + 2 * 18432)
    modall = sb("modall", [128, 2, 36], F32, PB + 16384 + 2 * 18432 + 160)
    for pc in range(4):
        s = pc % 2
        for jj in range(9):
            E.dma("pool", "ada%d" % s, lambda e, s=s, pc=pc, jj=jj: e.dma_start(out=adab[s][:, jj, :], in_=w_ada[:, pc * 9 + jj, :]),
                  writes=[("adab", s, jj)])
        E.retarget([("adab", s, jj) for jj in range(9)], "ada%d" % s)
        for jj in range(9):
            j = pc * 9 + jj
            for kc in range(8):
                E.op("pe", lambda e, s=s, jj=jj, kc=kc, j=j: e.matmul(
                    bank(2, 1, j), lhsT=adab[s][:, jj, kc * 128:(kc + 1) * 128], rhs=cact_bf[:, kc:kc + 1],
                    start=(kc == 0), stop=(kc == 7)),
                    reads=[("adab", s, jj), "cactbf"], writes=[("ps", 2)], signal=(kc == 7 and jj == 8))
    E.op("dve", lambda e: e.tensor_copy(out=modp[:], in_=bank(2, 36)), reads=[("ps", 2)], writes=["modp"])
    E.dma("pool", "cioD1", lambda e: e.dma_start(out=ccD_in[:, :], in_=modp[:]), reads=["modp"], writes=["ccD_in"])
    E.cc("cc", lambda e: e.collective_compute("AllGather", ALU.bypass, replica_groups=PAIRS, ins=[ccD_in[:, :]], outs=[ccD_out[:, :]]),
         reads=["ccD_in"], writes=["ccD_out"])
    xin = [sb("xin%d" % i, [128, D], F32, PB + i * 4096) for i in range(4)]
    for tk in range(16):
        s = tk % 4
        E.dma("sp", "xin%d" % s, lambda e, s=s, tk=tk: e.dma_start(out=xin[s][:], in_=x_in[tk * 128:(tk + 1) * 128, :]),
              reads=([("adab", 1, 8)] if tk >= 8 else []), writes=[("xin", s)])
        for hb in range(2):
            bk = (tk * 2 + hb) % 2
            for cc in range(4):
                c = hb * 4 + cc
                E.op("pe", lambda e, s=s, c=c, bk=bk, cc=cc: e.transpose(
                    out=bank(bk, 128, cc * 128), in_=xin[s][:, c * 128:(c + 1) * 128], identity=ident[:]),
                    reads=[("xin", s), "ident"], writes=[("ps", bk)], signal=(cc == 3))
            eng = "act"
            if eng == "act":
                E.op("act", lambda e, hb=hb, tk=tk, bk=bk: e.copy(
                    out=xT[:, hb * 4:(hb + 1) * 4, tk * 128:(tk + 1) * 128],
                    in_=bank(bk).rearrange("p (c n) -> p c n", c=4)),
                    reads=[("ps", bk)], writes=[("x", tk // 4)])
            else:
                E.op("dve", lambda e, hb=hb, tk=tk, bk=bk: e.tensor_copy(
                    out=xT[:, hb * 4:(hb + 1) * 4, tk * 128:(tk + 1) * 128],
                    in_=bank(bk).rearrange("p (c n) -> p c n", c=4)),
                    reads=[("ps", bk)], writes=[("x", tk // 4)])

    E.dma("pool", "cioD2", lambda e: e.dma_start(out=modall[:], in_=ccD_out.rearrange("(r p) n -> p r n", p=128)),
          reads=["ccD_out"], writes=["modall"])
    E.op("dve", lambda e: e.tensor_tensor(out=mod[:], in0=modall[:].rearrange("p r n -> p (r n)"), in1=badas[:], op=ALU.add),
         reads=["modall", "badas"], writes=[("mod", 0), ("mod", 1), ("mod", 2)])
    modv = mod[:].rearrange("p (s k c) -> p s k c", s=3, k=3)
    for s_ in range(3):
        E.op("dve", lambda e, s_=s_: e.scalar_tensor_tensor(
            out=gs32[:, s_, :], in0=modv[:, s_, 1, :], scalar=1.0, in1=npre_s[:, s_, :], op0=ALU.add, op1=ALU.mult),
            reads=[("mod", s_), "npre"], writes=[("gs32", s_)])
        E.op("dve", lambda e, s_=s_: e.tensor_scalar(out=gs32[:, s_, :], in0=gs32[:, s_, :], scalar1=32.0,
                                                      scalar2=None, op0=ALU.mult), reads=[("gs32", s_)], writes=[("gs32", s_)])
        rw = 32.0 * (1.0 if s_ == 1 else 0.5)
        E.op("dve", lambda e, s_=s_, rw=rw: e.scalar_tensor_tensor(
            out=gg32[:, s_, :], in0=modv[:, s_, 2, :], scalar=rw, in1=npost_s[:, s_, :], op0=ALU.mult, op1=ALU.mult),
            reads=[("mod", s_), "npost"], writes=[("gg32", s_)])

    def shiftv(s_, c):
        return modv[:, s_, 0, c:c + 1]

    E.barrier()

    def prenorm(s_, tiles, hT, hcol0, tmp, pos=None):
        xsq, t1, rstd = tmp
        for i0, t in enumerate(tiles):
            i = pos[i0] if pos is not None else i0
            sbk = 6 + (i % 2)
            for c in range(DC):
                q = c % 2
                if c % 2 == 0:
                    E.op("act", lambda e, c=c, t=t, q=q: e.activation(out=xsq[q][:], in_=xT[:, c, t * TT:(t + 1) * TT], func=AF.Square),
                         reads=[("x", t)], writes=[("xsq", q)])
                else:
                    E.op("dve", lambda e, c=c, t=t, q=q: e.tensor_tensor(out=xsq[q][:], in0=xT[:, c, t * TT:(t + 1) * TT],
                                                                           in1=xT[:, c, t * TT:(t + 1) * TT], op=ALU.mult),
                         reads=[("x", t)], writes=[("xsq", q)])
                E.op("pe", lambda e, c=c, q=q, sbk=sbk: e.matmul(bank(sbk), lhsT=ones_bf[:], rhs=xsq[q][:], start=(c == 0), stop=(c == 7)),
                     reads=[("xsq", q), "ones"], writes=[("ps", sbk)], signal=True)
            r = i % 2
            E.op("act", lambda e, r=r, sbk=sbk: e.activation(out=rstd[r][:], in_=bank(sbk), func=AF.Ln, bias=epsb[:, 0:1]),
                 reads=[("ps", sbk), "epsb"], writes=[("rstd", r)])
            E.op("act", lambda e, r=r: e.activation(out=rstd[r][:], in_=rstd[r][:], func=AF.Exp, scale=-0.5),
                 reads=[("rstd", r)], writes=[("rstd", r)])
            for c in range(DC):
                q = c % 2
                E.op("dve", lambda e, c=c, t=t, q=q, r=r: e.tensor_tensor(out=t1[q][:], in0=xT[:, c, t * TT:(t + 1) * TT], in1=rstd[r][:], op=ALU.mult),
                     reads=[("x", t), ("rstd", r)], writes=[("t1", q)])
                if c in (3, 7):
                    E.op("dve", lambda e, c=c, i=i, q=q: e.tensor_scalar(out=hT[:, c, hcol0 + i * TT: hcol0 + (i + 1) * TT], in0=t1[q][:],
                                                                          scalar1=gs32[:, s_, c:c + 1], scalar2=shiftv(s_, c),
                                                                          op0=ALU.mult, op1=ALU.add),
                         reads=[("t1", q), ("gs32", s_), ("mod", s_)], writes=[("h", i, c)])
                else:
                    E.op("act", lambda e, c=c, i=i, q=q: e.activation(out=hT[:, c, hcol0 + i * TT: hcol0 + (i + 1) * TT], in_=t1[q][:],
                                                                       func=AF.Identity, bias=shiftv(s_, c), scale=gs32[:, s_, c:c + 1]),
                         reads=[("t1", q), ("gs32", s_), ("mod", s_)], writes=[("h", i, c)])

    def postnorm_residual(s_, t, ytile, ysq_ready_bank, tmp, extra=None, steps=None):
        t1, rstd = tmp
        r = t % 2

        def st_rstd():
            E.op("act", lambda e, r=r, b=ysq_ready_bank: e.activation(out=rstd[r][:], in_=bank(b), func=AF.Ln, bias=epsb[:, 0:1]),
                 reads=[("ps", ysq_ready_bank), "epsb"], writes=[("rstd", r)])
            E.op("act", lambda e, r=r: e.activation(out=rstd[r][:], in_=rstd[r][:], func=AF.Exp, scale=-0.5),
                 reads=[("rstd", r)], writes=[("rstd", r)])

        def st_chunk(c):
            q = c % 2
            E.op("dve", lambda e, c=c, q=q, r=r: e.tensor_tensor(out=t1[q][:], in0=ytile(c), in1=rstd[r][:], op=ALU.mult),
                 reads=[("y", t, c), ("rstd", r)] + (extra(c) if extra else []), writes=[("t1", q)])
            E.op("dve", lambda e, c=c, q=q, t=t: e.scalar_tensor_tensor(
                out=xT[:, c, t * TT:(t + 1) * TT], in0=t1[q][:], scalar=gg32[:, s_, c:c + 1], in1=xT[:, c, t * TT:(t + 1) * TT],
                op0=ALU.mult, op1=ALU.add),
                reads=[("t1", q), ("gg32", s_), ("x", t)], writes=[("x", t)])

        todo = [st_rstd] + [(lambda c=c: st_chunk(c)) for c in range(DC)]
        if steps is None:
            for f_ in todo:
                f_()
        else:
            steps.extend(todo)

    out_state = {"bufs": None, "n": 0}

    def out_step(tk, hb, copy_eng="act", bank_base=4):
        k = out_state["n"] % 2; out_state["n"] += 1
        buf = out_state["bufs"][k]
        bk = bank_base + k
        for cc in range(4):
            c = hb * 4 + cc
            E.op("pe", lambda e, c=c, cc=cc: e.transpose(
                out=bank(bk, 128, cc * 128), in_=xT[:, c, tk * 128:(tk + 1) * 128], identity=ident[:]),
                reads=[("x", tk // 4), "ident"], writes=[("ps", bk)], signal=(cc == 3))
        if copy_eng == "act":
            E.op("act", lambda e: e.copy(out=buf[:], in_=bank(bk)), reads=[("ps", bk)], writes=[("ost", k)])
        else:
            E.op("dve", lambda e: e.tensor_copy(out=buf[:], in_=bank(bk)), reads=[("ps", bk)], writes=[("ost", k)])
        E.dma("sp", "ost%d" % k, lambda e: e.dma_start(out=out_d[tk * 128:(tk + 1) * 128, hb * 512:(hb + 1) * 512], in_=buf[:]),
              reads=[("ost", k)], writes=[("outd", tk, hb)])

    def ffn(s_, wgu_src, wdn_src):
        o2 = PB
        hT = sb("f_hT%d" % s_, [128, DC, 1024], BF16, o2); o2 += 16384
        aT = sb("f_aT%d" % s_, [128, FC, 1024], BF16, o2); o2 += 45056
        y = sb("f_y%d" % s_, [128, DC, 1024], F32, o2); o2 += 32768
        wgu = []
        for i in range(3):
            wgu.append(sb("f_wgu%d_%d" % (s_, i), [128, 8, 256], BF16, o2)); o2 += 4096
        wdn = []
        for i in range(2):
            wdn.append(sb("f_wdn%d_%d" % (s_, i), [128, FC, 128], BF16, o2)); o2 += 5632
        xsq = []; t1 = []; sg = []; ysq = []; rstd = []
        for i in range(2):
            xsq.append(sb("f_xsq%d_%d" % (s_, i), [128, TT], BF16, o2)); o2 += 1024
            t1.append(sb("f_t1%d_%d" % (s_, i), [128, TT], F32, o2)); o2 += 2048
            sg.append(sb("f_sg%d_%d" % (s_, i), [128, TT], F32, o2)); o2 += 2048
            ysq.append(sb("f_ysq%d_%d" % (s_, i), [128, TT], BF16, o2)); o2 += 1024
            rstd.append(sb("f_rstd%d_%d" % (s_, i), [128, TT], F32, o2)); o2 += 2048
        if s_ == 2:
            out_state["bufs"] = []
            for i in range(2):
                out_state["bufs"].append(sb("f_ost%d" % i, [128, 512], F32, o2)); o2 += 2048
        assert o2 <= SB_TOP, o2
        it = 0
        deferred = []
        prenorm(s_, [0, 1], hT, 0, (xsq, t1, rstd))
        for g in range(2):
            tiles = [2 * g, 2 * g + 1]
            for f in range(FC):
                ws = f % 3
                E.dma("pool", "wgu%d" % ws, lambda e, ws=ws, f=f: e.dma_start(out=wgu[ws][:], in_=wgu_src[f]),
                      writes=[("wgu", ws)])
                for i in range(2):
                    pb_ = it % 2; it += 1
                    for kc in range(8):
                        E.op("pe", lambda e, ws=ws, kc=kc, i=i, pb_=pb_: e.matmul(
                            bank(pb_), lhsT=wgu[ws][:, kc, 0:128], rhs=hT[:, kc, i * TT:(i + 1) * TT], start=(kc == 0), stop=(kc == 7)),
                            reads=[("wgu", ws), ("h", i, kc)], writes=[("ps", pb_)], signal=(kc == 7))
                    for kc in range(8):
                        E.op("pe", lambda e, ws=ws, kc=kc, i=i, pb_=pb_: e.matmul(
                            bank(2 + pb_), lhsT=wgu[ws][:, kc, 128:256], rhs=hT[:, kc, i * TT:(i + 1) * TT], start=(kc == 0), stop=(kc == 7)),
                            reads=[("wgu", ws), ("h", i, kc)], writes=[("ps", 2 + pb_)], signal=(kc == 7))
                    E.op("act", lambda e, pb_=pb_: e.activation(out=sg[pb_][:], in_=bank(pb_), func=AF.Silu),
                         reads=[("ps", pb_)], writes=[("sg", pb_)])
                    E.op("dve", lambda e, pb_=pb_, f=f, i=i: e.tensor_tensor(
                        out=aT[:, f, i * TT:(i + 1) * TT], in0=sg[pb_][:], in1=bank(2 + pb_), op=ALU.mult),
                        reads=[("sg", pb_), ("ps", 2 + pb_)], writes=[("a", i, f)])
                for _ in range(2):
                    if deferred:
                        deferred.pop(0)()
            while deferred:
                deferred.pop(0)()
            if g == 0:
                prenorm(s_, [2, 3], hT, 0, (xsq, t1, rstd))
            def down_block(d, i, ws):
                nonlocal it
                t = tiles[i]
                pb_ = 4 + (it % 2); it += 1
                for fc in range(FC):
                    E.op("pe", lambda e, fc=fc: e.matmul(
                        bank(pb_), lhsT=wdn[ws][:, fc, :], rhs=aT[:, fc, i * TT:(i + 1) * TT], start=(fc == 0), stop=(fc == FC - 1)),
                        reads=[("wdn", ws, fc // 11), ("a", i, fc)], writes=[("ps", pb_)], signal=(fc == FC - 1))
                q = pb_ % 2
                E.op("act", lambda e: e.activation(out=ysq[q][:], in_=bank(pb_), func=AF.Square),
                     reads=[("ps", pb_)], writes=[("ysq", q)])
                E.op("act", lambda e: e.copy(out=y[:, d, i * TT:(i + 1) * TT], in_=bank(pb_)),
                     reads=[("ps", pb_)], writes=[("y", t, d), ("ybuf", i, d)])
                E.op("pe", lambda e: e.matmul(bank(6 + i), lhsT=ones_bf[:], rhs=ysq[q][:], start=(d == 0), stop=(d == 7)),
                     reads=[("ysq", q), "ones"], writes=[("ps", 6 + i)], signal=True)

            def load_wdn(d, ws):
                for hf in range(2):
                    E.dma("pool", "wdn%d" % ws, lambda e, hf=hf: e.dma_start(
                        out=wdn[ws][:, hf * 11:(hf + 1) * 11, :], in_=wdn_src[d, :, hf * 11:(hf + 1) * 11, :]),
                        writes=[("wdn", ws, hf)])
                E.retarget([("wdn", ws, 0), ("wdn", ws, 1)], "wdn%d" % ws)

            def pn(i, steps=None):
                postnorm_residual(s_, tiles[i], lambda c: y[:, c, i * TT:(i + 1) * TT], 6 + i, (t1, rstd),
                                  extra=lambda c: [("ybuf", i, c)], steps=steps)

            if g == 0:
                for d in range(DC):
                    ws = d % 2
                    load_wdn(d, ws)
                    for i in range(2):
                        down_block(d, i, ws)
                pn(0, deferred); pn(1, deferred)
                if s_ == 2:
                    for tk in range(8):
                        for hb in range(2):
                            deferred.append(lambda tk=tk, hb=hb: out_step(tk, hb, "act", 4))
            else:
                pend = []
                wn = 0
                for i in range(2):
                    for d in range(DC):
                        ws = wn % 2; wn += 1
                        load_wdn(d, ws)
                        down_block(d, i, ws)
                        for _ in range(4):
                            if pend:
                                pend.pop(0)()
                    if i == 0:
                        pn(0, pend)
                        if s_ == 2:
                            for tk in range(8, 12):
                                for hb in range(2):
                                    pend.append(lambda tk=tk, hb=hb: out_step(tk, hb, "act", 0))
                while pend:
                    pend.pop(0)()
                pn(1)
                if s_ == 2:
                    for tk in range(12, 16):
                        out_step(tk, 0, "act", 0); out_step(tk, 1, "dve", 0)
        E.barrier()

    def mixer():
        s_ = 1
        o2 = PB
        hT = sb("m_hT", [128, DC, T], BF16, o2); o2 += 32768
        R0 = sb("m_R0", [128, DC, T], BF16, o2); R0_OFF = o2; o2 += 32768
        R1 = sb("m_R1", [128, DC, T], BF16, o2); o2 += 32768
        QOFF = o2
        qT = sb("m_qT", [128, 4, T], BF16, o2); o2 += 16384
        VOFF = o2
        FREE_END = SB_TOP
        ot = VOFF
        xsq = []; t1 = []; rstd = []
        for i in range(2):
            xsq.append(sb("m_xsq%d" % i, [128, TT], BF16, ot)); ot += 1024
            t1.append(sb("m_t1%d" % i, [128, TT], F32, ot)); ot += 2048
            rstd.append(sb("m_rstd%d" % i, [128, TT], F32, ot)); ot += 2048
        prenorm(s_, [3], hT, 0, (xsq, t1, rstd), pos=[3])
        win = []
        for i in range(3):
            win.append(sb("m_win%d" % i, [128, 8, 128], BF16, QOFF + i * 2048))
        wcount = [0]

        def load_win(oc):
            ws = wcount[0] % 3; wcount[0] += 1
            wt = win[ws]
            E.dma("pool", "win%d" % ws, lambda e, wt=wt, oc=oc: e.dma_start(out=wt[:], in_=win_d[oc]), writes=[("win", ws)])
            return (wt, ("win", ws))

        for c in range(DC):
            ws = load_win(12 + c)
            for kc in range(8):
                E.op("pe", lambda e, ws=ws, kc=kc, c=c: e.matmul(bank(0, 3, c * 4), lhsT=ws[0][:, kc, :], rhs=hT[:, kc, T - 3:T],
                                                                  start=(kc == 0), stop=(kc == 7)),
                     reads=[ws[1], ("h", 3, kc)], writes=[("ps", 0)], signal=(kc == 7))
        E.op("dve", lambda e: e.memset(xrt[:].rearrange("p c j -> p (c j)"), 0.0), writes=["xrt"])
        E.op("dve", lambda e: e.tensor_copy(out=xrt[:, :, 0:3], in_=bank(0, 32).rearrange("p (c j) -> p c j", j=4)[:, :, 0:3]),
             reads=[("ps", 0)], writes=["xrt"])
        E.dma("pool", "cioA1", lambda e: e.dma_start(out=ccA_in[:, :], in_=xrt[:].rearrange("p c j -> p (c j)")),
              reads=["xrt"], writes=["ccA_in"])
        E.cc("cc", lambda e: e.collective_compute("AllGather", ALU.bypass, replica_groups=PAIRS, ins=[ccA_in[:, :]], outs=[ccA_out[:, :]]),
             reads=["ccA_in"], writes=["ccA_out"])
        ot = QOFF + 3 * 2048
        xrb = sb("m_xrb", [128, 4 + T], F32, ot); ot += (4 + T) * 4
        ot = (ot + 31) // 32 * 32
        names = ["U", "h0", "P"]
        L = {}
        for n_ in names:
            L[n_] = sb("m_L" + n_, [128, TT], F32, ot); ot += 2048
        xc2 = []; mu2 = []
        for i in range(2):
            xc2.append(sb("m_xc%d" % i, [128, TT], F32, ot)); ot += 2048
            mu2.append(sb("m_mu%d" % i, [128, TT], F32, ot)); ot += 2048
        xcb = sb("m_xcb", [128, TT], BF16, ot); ot += 1024
        gyb = sb("m_gyb", [128, T], F32, ot); ot += T * 4
        wblk = sb("m_wblk", [128, 2, DC, 128], BF16, ot); ot += 4096
        assert ot <= FREE_END, ot
        E.op("dve", lambda e: e.memset(wblk[:].rearrange("p a c n -> p (a c n)"), 0.0), writes=["wblk"])
        for a, src in enumerate((lwa_d, lwx_d)):
            for b in range(2):
                E.dma("pool", "wblk", lambda e, a=a, b=b, src=src: e.dma_start(
                    out=wblk[b * 64:(b + 1) * 64, a, :, b * 64:(b + 1) * 64], in_=src[b * 64:(b + 1) * 64, :, :]),
                    reads=["wblk"], writes=[("wblkd", a, b)])
        E.retarget([("wblkd", a, b) for a in range(2) for b in range(2)], "wblk")
        nxt0 = (load_win(12), load_win(20))
        prenorm(s_, [0, 1, 2], hT, 0, (xsq, t1, rstd), pos=[0, 1, 2])
        E.barrier()

        E.dma("pool", "cioA2", lambda e: e.dma_start(out=xrh_raw[:].rearrange("p c j -> p (c j)"), in_=ccA_out[0:128, :]),
              reads=["ccA_out"], writes=["xrh_raw"])
        E.op("dve", lambda e: e.tensor_scalar(out=xrh[:].rearrange("p c j -> p (c j)"), in0=xrh_raw[:].rearrange("p c j -> p (c j)"),
                                               scalar1=flag_s[:, 0:1], scalar2=None, op0=ALU.mult),
             reads=["xrh_raw", "flag"], writes=["xrh"])

        GB = [(2, 3), (6, 7)]

        def lru_G1(c, t, ws_y):
            pby = 4 + (t % 2)
            for kc in range(8):
                E.op("pe", lambda e, kc=kc: e.matmul(
                    bank(pby), lhsT=ws_y[0][:, kc, :], rhs=hT[:, kc, t * TT:(t + 1) * TT], start=(kc == 0), stop=(kc == 7)),
                    reads=[ws_y[1], ("h", t, kc)], writes=[("ps", pby)], signal=(kc == 7))
            E.op("act", lambda e: e.activation(out=gyb[:, t * TT:(t + 1) * TT], in_=bank(pby), func=AF.Gelu_apprx_tanh),
                 reads=[("ps", pby)], writes=[("gy", t)])

        def lru_F1(c, t, ws_x):
            pbk = t % 2
            for kc in range(8):
                E.op("pe", lambda e, kc=kc, pbk=pbk: e.matmul(
                    bank(pbk), lhsT=ws_x[0][:, kc, :], rhs=hT[:, kc, t * TT:(t + 1) * TT], start=(kc == 0), stop=(kc == 7)),
                    reads=[ws_x[1], ("h", t, kc)], writes=[("ps", pbk)], signal=(kc == 7))
            E.op("act", lambda e: e.copy(out=xrb[:, 4 + t * TT: 4 + (t + 1) * TT], in_=bank(pbk)),
                 reads=[("ps", pbk)], writes=[("xrb", t)])

        def lru_F2(c, t):
            ga, gi = GB[t % 2]
            xc_ = xc2[t % 2]; mu_ = mu2[t % 2]
            base = 4 + t * TT
            E.op("dve", lambda e: e.tensor_scalar(
                out=xc_[:], in0=xrb[:, base: base + TT], scalar1=convw_s[:, c, 3:4], scalar2=convb_s[:, c:c + 1],
                op0=ALU.mult, op1=ALU.add),
                reads=[("xrb", t), "convw", "convb"], writes=[("xc", t % 2)])
            for j in (2, 1, 0):
                E.op("dve", lambda e, j=j: e.scalar_tensor_tensor(
                    out=xc_[:], in0=xrb[:, base - 3 + j: base - 3 + j + TT], scalar=convw_s[:, c, j:j + 1], in1=xc_[:],
                    op0=ALU.mult, op1=ALU.add),
                    reads=[("xrb", t), ("xrb", t - 1), ("xc", t % 2)], writes=[("xc", t % 2)])
            E.op("act", lambda e: e.copy(out=xcb[:], in_=xc_[:]), reads=[("xc", t % 2)], writes=["xcb"])
            E.op("pe", lambda e: e.matmul(bank(ga), lhsT=wblk[:, 0, c, :], rhs=xcb[:], start=True, stop=True),
                 reads=["wblk", ("wblkd", 0, 0), ("wblkd", 0, 1), "xcb"], writes=[("ps", ga)], signal=True)
            E.op("pe", lambda e: e.matmul(bank(gi), lhsT=wblk[:, 1, c, :], rhs=xcb[:], start=True, stop=True),
                 reads=["wblk", ("wblkd", 1, 0), ("wblkd", 1, 1), "xcb"], writes=[("ps", gi)], signal=True)
            E.op("act", lambda e: e.activation(out=bank(ga), in_=bank(ga), func=AF.Tanh, bias=hba[:, c:c + 1], scale=0.5),
                 reads=["hba"], writes=[("ps", ga)])
            E.op("act", lambda e: e.activation(out=bank(gi), in_=bank(gi), func=AF.Tanh, bias=hbx[:, c:c + 1], scale=0.5),
                 reads=["hbx"], writes=[("ps", gi)])
            E.op("act", lambda e: e.activation(out=mu_[:], in_=bank(ga), func=AF.Exp, bias=nsp[:, c:c + 1], scale=nsp[:, c:c + 1]),
                 reads=[("ps", ga), "nsp"], writes=[("mu", t % 2)])
            E.op("act", lambda e: e.activation(out=bank(ga), in_=bank(ga), func=AF.Exp, bias=hnsp[:, c:c + 1], scale=hnsp[:, c:c + 1]),
                 reads=["hnsp"], writes=[("ps", ga)])
            E.op("act", lambda e: e.activation(out=mu_[:], in_=mu_[:], func=AF.Ln, bias=qrt[:, 0:1], scale=-0.25),
                 reads=["qrt"], writes=[("mu", t % 2)])
            E.op("act", lambda e: e.activation(out=mu_[:], in_=mu_[:], func=AF.Exp, scale=0.5),
                 reads=[], writes=[("mu", t % 2)])

        def lru_B(c, t):
            ga, gi = GB[t % 2]
            xc_ = xc2[t % 2]; mu_ = mu2[t % 2]
            E.op("dve", lambda e: e.scalar_tensor_tensor(out=L["U"][:], in0=bank(gi), scalar=1.0, in1=xc_[:], op0=ALU.add, op1=ALU.mult),
                 reads=[("ps", gi), ("xc", t % 2)], writes=["U"])
            E.op("dve", lambda e: e.tensor_tensor(out=L["U"][:], in0=L["U"][:], in1=mu_[:], op=ALU.mult),
                 reads=[("mu", t % 2)], writes=["U"])
            ini_h = 0.0 if t == 0 else hend[:, c:c + 1]
            ini_p = 1.0 if t == 0 else hin_raw[:, c:c + 1]
            E.op("dve", lambda e: e.tensor_tensor_scan(out=L["h0"][:], data0=bank(ga), data1=L["U"][:], initial=ini_h,
                                                       op0=ALU.mult, op1=ALU.add),
                 reads=[("ps", ga), "U", "hend"], writes=["h0"])
            E.op("dve", lambda e: e.tensor_tensor_scan(out=L["P"][:], data0=bank(ga), data1=zeros5[:], initial=ini_p,
                                                       op0=ALU.mult, op1=ALU.add),
                 reads=[("ps", ga), "zeros5", "pend"], writes=["P"])
            E.op("dve", lambda e: e.tensor_copy(out=hend[:, c:c + 1], in_=L["h0"][:, TT - 1:TT]), reads=["h0"], writes=["hend"])
            E.op("dve", lambda e: e.tensor_copy(out=hin_raw[:, c:c + 1], in_=L["P"][:, TT - 1:TT]), reads=["P"], writes=["pend"])
            E.op("pool", lambda e: e.tensor_tensor(out=R0[:, c, t * TT:(t + 1) * TT], in0=L["h0"][:], in1=gyb[:, t * TT:(t + 1) * TT], op=ALU.mult),
                 reads=["h0", ("gy", t)], writes=[("R0", c)])
            E.op("pool", lambda e: e.tensor_tensor(out=R1[:, c, t * TT:(t + 1) * TT], in0=L["P"][:], in1=gyb[:, t * TT:(t + 1) * TT], op=ALU.mult),
                 reads=["P", ("gy", t)], writes=[("R1", c)])

        nxt = nxt0
        for t in range(NT):
            lru_G1(0, t, nxt[1])
        for c in range(DC):
            ws_x, ws_y = nxt
            E.op("act", lambda e, c=c: e.copy(out=xrb[:, 1:4], in_=xrh[:, c, 0:3]), reads=["xrh"], writes=[("xrb", -1)])
            lru_F1(c, 0, ws_x)
            lru_F1(c, 1, ws_x)
            lru_F2(c, 0)
            lru_F1(c, 2, ws_x)
            lru_F2(c, 1)
            lru_B(c, 0)
            lru_F1(c, 3, ws_x)
            lru_F2(c, 2)
            lru_B(c, 1)
            lru_F2(c, 3)
            if c + 1 < DC:
                nxt = (load_win(12 + c + 1), load_win(20 + c + 1))
            lru_B(c, 2)
            if c + 1 < DC:
                lru_G1(c + 1, 0, nxt[1]); lru_G1(c + 1, 1, nxt[1]); lru_G1(c + 1, 2, nxt[1])
            lru_B(c, 3)
            if c + 1 < DC:
                lru_G1(c + 1, 3, nxt[1])
        E.barrier()
        E.dma("pool", "cioB1", lambda e: e.dma_start(out=ccB_in[:, :], in_=hend[:]), reads=["hend"], writes=["ccB_in"])
        E.cc("cc", lambda e: e.collective_compute("AllGather", ALU.bypass, replica_groups=PAIRS, ins=[ccB_in[:, :]], outs=[ccB_out[:, :]]),
             reads=["ccB_in"], writes=["ccB_out"])

        ot = VOFF
        win = []
        for i in range(3):
            win.append(sb("m2_win%d" % i, [128, 8, 128], BF16, ot)); ot += 2048
        wrc = []
        for i in range(2):
            wrc.append(sb("m2_wrc%d" % i, [128, 8, 128], BF16, ot)); ot += 2048
        sgt = []
        for i in range(2):
            sgt.append(sb("m2_sg%d" % i, [128, TT], F32, ot)); ot += 2048
        assert ot <= FREE_END
        it = 0
        for oc in range(4):
            ws = load_win(oc)
            for t in range(NT):
                pbk = it % 2; it += 1
                for kc in range(8):
                    E.op("pe", lambda e, ws=ws, kc=kc, t=t, pbk=pbk: e.matmul(
                        bank(pbk), lhsT=ws[0][:, kc, :], rhs=hT[:, kc, t * TT:(t + 1) * TT], start=(kc == 0), stop=(kc == 7)),
                        reads=[ws[1], ("h", t, kc)], writes=[("ps", pbk)], signal=(kc == 7))
                E.op("act", lambda e, oc=oc, t=t, pbk=pbk: e.activation(out=qT[:, oc, t * TT:(t + 1) * TT], in_=bank(pbk), func=AF.Copy, scale=0.125),
                     reads=[("ps", pbk)], writes=[("q", oc)])
        E.dma("pool", "cioB2", lambda e: e.dma_start(out=hin_raw[:], in_=ccB_out[0:128, :]), reads=["ccB_out", "pend"], writes=["hin_raw"])
        E.op("dve", lambda e: e.tensor_scalar(out=hin[:], in0=hin_raw[:], scalar1=flag_s[:, 0:1], scalar2=None, op0=ALU.mult),
             reads=["hin_raw", "flag"], writes=["hin"])
        for c in range(DC):
            E.op("dve", lambda e, c=c: e.scalar_tensor_tensor(out=R0[:, c, :], in0=R1[:, c, :], scalar=hin[:, c:c + 1], in1=R0[:, c, :],
                                                             op0=ALU.mult, op1=ALU.add),
                 reads=[("R0", c), ("R1", c), "hin"], writes=[("R0", c)])
        Z = R1
        for oc in range(DC):
            wr = oc % 2
            E.dma("pool", "wrc%d" % wr, lambda e, wr=wr, oc=oc: e.dma_start(out=wrc[wr][:], in_=wrec_d[oc]), writes=[("wrc", wr)])
            ws = load_win(36 + oc)
            for t in range(NT):
                pbk = it % 2; it += 1
                for c in range(8):
                    E.op("pe", lambda e, wr=wr, c=c, t=t, pbk=pbk: e.matmul(
                        bank(pbk), lhsT=wrc[wr][:, c, :], rhs=R0[:, c, t * TT:(t + 1) * TT], start=(c == 0), stop=(c == 7)),
                        reads=[("wrc", wr), ("R0", c)], writes=[("ps", pbk)], signal=(c == 7))
                for kc in range(8):
                    E.op("pe", lambda e, ws=ws, kc=kc, t=t, pbk=pbk: e.matmul(
                        bank(2 + pbk), lhsT=ws[0][:, kc, :], rhs=hT[:, kc, t * TT:(t + 1) * TT], start=(kc == 0), stop=(kc == 7)),
                        reads=[ws[1], ("h", t, kc)], writes=[("ps", 2 + pbk)], signal=(kc == 7))
                E.op("act", lambda e, pbk=pbk: e.activation(out=sgt[pbk][:], in_=bank(2 + pbk), func=AF.Sigmoid),
                     reads=[("ps", 2 + pbk)], writes=[("sgt", pbk)])
                E.op("dve", lambda e, oc=oc, t=t, pbk=pbk: e.tensor_tensor(out=Z[:, oc, t * TT:(t + 1) * TT], in0=sgt[pbk][:], in1=bank(pbk), op=ALU.mult),
                     reads=[("sgt", pbk), ("ps", pbk)], writes=[("Z", oc, t), ("R1", oc)])
        E.barrier()

        kT = sb("m_kT", [128, 4, 2560], BF16, R0_OFF)
        biasg = sb("m_biasg", [128, 8, 384], F32, R0_OFF + 20480)
        v = sb("m_v", [128, 20, 512], BF16, VOFF)
        ot = VOFF + 20480
        win = [sb("m1_win%d" % i, [128, 8, 128], BF16, R0_OFF + 20480 + i * 2048) for i in range(3)]
        for oc in range(4, 8):
            ws = load_win(oc)
            for t in range(NT):
                pbk = it % 2; it += 1
                for kc in range(8):
                    E.op("pe", lambda e, ws=ws, kc=kc, t=t, pbk=pbk: e.matmul(
                        bank(pbk), lhsT=ws[0][:, kc, :], rhs=hT[:, kc, t * TT:(t + 1) * TT], start=(kc == 0), stop=(kc == 7)),
                        reads=[ws[1], ("h", t, kc)], writes=[("ps", pbk)], signal=(kc == 7))
                E.op("act", lambda e, oc=oc, t=t, pbk=pbk: e.copy(out=kT[:, oc - 4, 512 + t * TT: 512 + (t + 1) * TT], in_=bank(pbk)),
                     reads=[("ps", pbk)], writes=[("k", oc - 4, 1 + t)])
        for hp in range(4):
            ws = load_win(8 + hp)
            for tq in range(4):
                pbk = it % 2; it += 1
                for tk4 in range(4):
                    tk = tq * 4 + tk4
                    for kc in range(8):
                        E.op("pe", lambda e, ws=ws, kc=kc, tk=tk, tk4=tk4, pbk=pbk: e.matmul(
                            bank(pbk, 128, tk4 * 128), lhsT=hT[:, kc, tk * 128:(tk + 1) * 128], rhs=ws[0][:, kc, :],
                            start=(kc == 0), stop=(kc == 7)),
                            reads=[ws[1], ("h", tk // 4, kc)], writes=[("ps", pbk)], signal=(kc == 7 and tk4 == 3))
                E.op("act", lambda e, hp=hp, tq=tq, pbk=pbk: e.copy(
                    out=v[:, 4 + tq * 4: 4 + tq * 4 + 4, hp * 128:(hp + 1) * 128], in_=bank(pbk).rearrange("p (a n) -> p a n", a=4)),
                    reads=[("ps", pbk)], writes=[("v", 1 + tq)])
        E.barrier()
        for oc in range(4):
            E.dma("pool", "cioC1", lambda e, oc=oc: e.dma_start(out=ccC_in[:, oc * 512:(oc + 1) * 512], in_=kT[:, oc, 2048:2560]),
                  reads=[("k", oc, 4)], writes=[("ccC_in", oc)])
        E.dma("pool", "cioC1", lambda e: e.dma_start(out=ccC_in[:, 2048:4096], in_=v[:, 16:20, :].rearrange("p a n -> p (a n)")),
              reads=[("v", 4)], writes=[("ccC_in", 4)])
        E.retarget([("ccC_in", i) for i in range(5)], "cioC1")
        E.cc("cc", lambda e: e.collective_compute("AllGather", ALU.bypass, replica_groups=PAIRS, ins=[ccC_in[:, :]], outs=[ccC_out[:, :]]),
             reads=[("ccC_in", i) for i in range(5)], writes=["ccC_out"])

        def load_halo():
            for oc in range(4):
                E.dma("pool", "cioC2", lambda e, oc=oc: e.dma_start(out=kT[:, oc, 0:512], in_=ccC_out[0:128, oc * 512:(oc + 1) * 512]),
                      reads=["ccC_out"], writes=[("k", oc, 0)])
            E.dma("pool", "cioC2", lambda e: e.dma_start(out=v[:, 0:4, :].rearrange("p a n -> p (a n)"), in_=ccC_out[0:128, 2048:4096]),
                  reads=["ccC_out"], writes=[("v", 0)])
            E.retarget([("k", oc, 0) for oc in range(4)] + [("v", 0)], "cioC2")
            E.op("dve", lambda e: e.tensor_scalar(out=v[:, 0:4, :].rearrange("p a n -> p (a n)"), in0=v[:, 0:4, :].rearrange("p a n -> p (a n)"),
                                                   scalar1=flag_s[:, 0:1], scalar2=None, op0=ALU.mult),
                 reads=[("v", 0), "flag"], writes=[("v", 0)])

        E.dma("sp", "biasg", lambda e: e.dma_start(out=biasg[:].rearrange("p h n -> p (h n)"), in_=biasg_d.rearrange("p h n -> p (h n)")),
              writes=["biasg"])

        for h in range(8):
            E.op("dve", lambda e, h=h: e.tensor_scalar(out=biasg[:, h, :], in0=biasg[:, h, :], scalar1=biasc_s[:, h:h + 1], scalar2=None,
                                                        op0=ALU.subtract), reads=["biasg", "biasc"], writes=["biasg"])
        ET = []; rc = []
        for i in range(2):
            ET.append(sb("m4_ET%d" % i, [128, 2, 640], BF16, ot)); ot += 2560
        for i in range(2):
            rc.append(sb("m4_rc%d" % i, [128, 128], F32, ot)); ot += 512
        assert ot <= FREE_END, ot
        attT = qT
        ROLE_SLOT = {1: 0, 2: 1, 0: 2, 3: 3, 4: 4}
        items = [(j, hp) for j in list(range(4, 16)) + list(range(4)) for hp in range(4)]

        def stageA(idx):
            j, hp = items[idx]
            sl = idx % 2
            base = 1536 * sl
            pkeys = [("ps", 3 * sl), ("ps", 3 * sl + 1), ("ps", 3 * sl + 2)]
            for r in range(5):
                slot = ROLE_SLOT[r]
                kt = j + r
                for hh in range(2):
                    p0 = hh * 64
                    E.op("pe", lambda e, p0=p0, kt=kt, slot=slot, hh=hh: e.matmul(
                        psum[:, base + hh * 768 + slot * 128: base + hh * 768 + (slot + 1) * 128],
                        lhsT=kT[p0:p0 + 64, hp, kt * 128:(kt + 1) * 128], rhs=qT[p0:p0 + 64, hp, j * 128:(j + 1) * 128],
                        start=True, stop=True),
                        reads=[("k", hp, kt // 4), ("q", hp)], writes=pkeys, signal=(r == 4 and hh == 1))
            sview = psum[:, base: base + 1536].rearrange("p (h n) -> p h n", h=2)
            E.op("dve", lambda e: e.tensor_tensor(
                out=sview[:, :, 256:640], in0=sview[:, :, 256:640], in1=biasg[:, 2 * hp:2 * hp + 2, :], op=ALU.add),
                reads=["biasg"], writes=pkeys)
            E.op("act", lambda e: e.activation(out=ET[sl][:], in_=sview[:, :, 0:640], func=AF.Exp),
                 reads=pkeys, writes=[("ET", sl)])

        def stageB(idx):
            j, hp = items[idx]
            sl = idx % 2
            pvb = 6 + sl
            for r in range(5):
                slot = ROLE_SLOT[r]; kt = j + r
                for hh in range(2):
                    p0 = hh * 64; h = 2 * hp + hh
                    E.op("pe", lambda e, kt=kt, h=h, p0=p0, slot=slot, r=r, hh=hh: e.matmul(
                        psum[p0:p0 + 64, pvb * 512: pvb * 512 + 128], lhsT=v[:, kt, h * 64:(h + 1) * 64], rhs=ET[sl][:, hh, slot * 128:(slot + 1) * 128],
                        start=(r == 0), stop=(r == 4)),
                        reads=[("v", kt // 4), ("ET", sl)], writes=[("ps", pvb)], signal=False)
            for r in range(5):
                slot = ROLE_SLOT[r]; kt = j + r
                on = flagones if kt < 4 else ones_bf
                for hh in range(2):
                    p0 = hh * 64
                    E.op("pe", lambda e, p0=p0, slot=slot, r=r, hh=hh, on=on: e.matmul(
                        psum[p0:p0 + 64, pvb * 512 + 128: pvb * 512 + 256], lhsT=on[:, 0:64], rhs=ET[sl][:, hh, slot * 128:(slot + 1) * 128],
                        start=(r == 0), stop=(r == 4)),
                        reads=["ones", "flagones", ("ET", sl)], writes=[("ps", pvb)], signal=(r == 4 and hh == 1))
            E.op("act", lambda e: e.activation(out=rc[sl][:], in_=psum[:, pvb * 512 + 128: pvb * 512 + 256], func=AF.Ln),
                 reads=[("ps", pvb)], writes=[("rc", sl)])
            E.op("act", lambda e: e.activation(out=rc[sl][:], in_=rc[sl][:], func=AF.Exp, scale=-1.0),
                 reads=[], writes=[("rc", sl)])
            E.op("dve", lambda e: e.tensor_tensor(
                out=attT[:, hp, j * 128:(j + 1) * 128], in0=psum[:, pvb * 512: pvb * 512 + 128], in1=rc[sl][:], op=ALU.mult),
                reads=[("ps", pvb), ("rc", sl), ("q", hp)], writes=[("att", hp, j // 4)])

        n_it = len(items)
        first_halo = min(i for i, (j, hp) in enumerate(items) if j < 4)
        stageA(0)
        for idx in range(n_it):
            if idx + 1 < n_it:
                if idx + 1 == first_halo:
                    load_halo()
                stageA(idx + 1)
            stageB(idx)
        E.barrier()

        ot = R0_OFF
        wout = sb("m6_wout", [128, DC, DC, 128], BF16, ot); ot += 16384
        ytile = sb("m6_y", [128, DC, TT], F32, ot); ot += 16384
        ot = VOFF
        win = [None] * 3
        for i in range(3):
            win[i] = sb("m6_win%d" % i, [128, 8, 128], BF16, ot); ot += 2048
        wat = []
        for i in range(2):
            wat.append(sb("m6_wat%d" % i, [128, 4, 128], BF16, ot)); ot += 1024
        sga = []; m1 = []; ysq = []; t1b = []; rstdb = []
        for i in range(2):
            sga.append(sb("m6_sga%d" % i, [128, TT], F32, ot)); ot += 2048
            m1.append(sb("m6_m1%d" % i, [128, TT], F32, ot)); ot += 2048
            ysq.append(sb("m6_ysq%d" % i, [128, TT], BF16, ot)); ot += 1024
            t1b.append(sb("m6_t1%d" % i, [128, TT], F32, ot)); ot += 2048
            rstdb.append(sb("m6_rstd%d" % i, [128, TT], F32, ot)); ot += 2048
        assert ot <= FREE_END, ot
        for dcx in range(DC):
            E.dma("pool", "wout", lambda e, dcx=dcx: e.dma_start(out=wout[:, dcx, :, :], in_=wout_d[dcx]), writes=[("wout", dcx)])
        E.retarget([("wout", dcx) for dcx in range(DC)], "wout")
        merged = Z
        it = 0
        for oc in range(DC):
            wa = oc % 2
            E.dma("pool", "wat%d" % wa, lambda e, wa=wa, oc=oc: e.dma_start(out=wat[wa][:], in_=watt_d[oc]), writes=[("wat", wa)])
            ws = load_win(28 + oc)
            for t in range(NT):
                pbk = it % 2; it += 1
                for hc in range(4):
                    E.op("pe", lambda e, wa=wa, hc=hc, t=t, pbk=pbk: e.matmul(
                        bank(pbk), lhsT=wat[wa][:, hc, :], rhs=attT[:, hc, t * TT:(t + 1) * TT], start=(hc == 0), stop=(hc == 3)),
                        reads=[("wat", wa), ("att", hc, t)], writes=[("ps", pbk)], signal=(hc == 3))
                for kc in range(8):
                    E.op("pe", lambda e, ws=ws, kc=kc, t=t, pbk=pbk: e.matmul(
                        bank(2 + pbk), lhsT=ws[0][:, kc, :], rhs=hT[:, kc, t * TT:(t + 1) * TT], start=(kc == 0), stop=(kc == 7)),
                        reads=[ws[1], ("h", t, kc)], writes=[("ps", 2 + pbk)], signal=(kc == 7))
                E.op("act", lambda e, pbk=pbk: e.activation(out=sga[pbk][:], in_=bank(2 + pbk), func=AF.Sigmoid),
                     reads=[("ps", 2 + pbk)], writes=[("sga", pbk)])
                E.op("dve", lambda e, pbk=pbk: e.tensor_tensor(out=m1[pbk][:], in0=sga[pbk][:], in1=bank(pbk), op=ALU.mult),
                     reads=[("sga", pbk), ("ps", pbk)], writes=[("m1", pbk)])
                E.op("dve", lambda e, oc=oc, t=t, pbk=pbk: e.tensor_tensor(
                    out=merged[:, oc, t * TT:(t + 1) * TT], in0=m1[pbk][:], in1=Z[:, oc, t * TT:(t + 1) * TT], op=ALU.add),
                    reads=[("m1", pbk), ("Z", oc, t)], writes=[("mg", oc, t)])
        E.barrier()
        ytile2 = sb("m6_y2", [128, DC, TT], F32, PB)
        yts = [ytile, ytile2]
        for t in range(NT):
            yt_ = yts[t % 2]
            for dcx in range(DC):
                pbk = 4 + (it % 2); it += 1
                for c in range(8):
                    E.op("pe", lambda e, dcx=dcx, c=c, t=t, pbk=pbk: e.matmul(
                        bank(pbk), lhsT=wout[:, dcx, c, :], rhs=merged[:, c, t * TT:(t + 1) * TT], start=(c == 0), stop=(c == 7)),
                        reads=[("wout", dcx), ("mg", c, t)], writes=[("ps", pbk)], signal=(c == 7))
                q = pbk % 2
                E.op("act", lambda e, pbk=pbk, q=q: e.activation(out=ysq[q][:], in_=bank(pbk), func=AF.Square),
                     reads=[("ps", pbk)], writes=[("ysq", q)])
                E.op("act", lambda e, pbk=pbk, dcx=dcx, yt_=yt_: e.copy(out=yt_[:, dcx, :], in_=bank(pbk)),
                     reads=[("ps", pbk)], writes=[("y", t, dcx), ("ybuf", t % 2, dcx)])
                E.op("pe", lambda e, q=q, t=t, dcx=dcx: e.matmul(bank(6 + t % 2), lhsT=ones_bf[:], rhs=ysq[q][:], start=(dcx == 0), stop=(dcx == 7)),
                     reads=[("ysq", q), "ones"], writes=[("ps", 6 + t % 2)], signal=True)
            postnorm_residual(s_, t, lambda c, yt_=yt_: yt_[:, c, :], 6 + t % 2, (t1b, rstdb), extra=lambda c, t=t: [("ybuf", t % 2, c)])
        E.barrier()

    ffn(0, wgu_d[0], wdn_d[0])
    mixer()
    ffn(2, wgu_d[1], wdn_d[1])

    E.final_wait("sp", ["ost0", "ost1"])

    import contextlib
    with contextlib.ExitStack() as stack:
        for name in sorted(E.sem_names):
            sems[name] = stack.enter_context(nc.semaphore(name))
        block = stack.enter_context(nc.Block())

        @block.tensor
        def _(e):
            E.replay("pe", e, sems)

        @block.scalar
        def _(e):
            E.replay("act", e, sems)

        @block.vector
        def _(e):
            E.replay("dve", e, sems)

        @block.gpsimd
        def _(e):
            E.replay("pool", e, sems)

        @block.sync
        def _(e):
            E.replay("sp", e, sems)
    return nc


_NC_CACHE = {}


def _prep_shared(inp):
    f = lambda a: np.ascontiguousarray(np.asarray(a, dtype=np.float32))
    sh = {}
    wa = f(inp["w_ada"])[0]
    sh["_w_ada_halves"] = f(wa.reshape(8, 128, 2, 36, 128).transpose(2, 1, 3, 0, 4).reshape(2, 128, 36, 1024))
    sh["b_ada_r"] = f(f(inp["b_ada"])[0].reshape(72, 128).T)
    sh["npre_r"] = f(f(inp["norm_pre"])[0].reshape(3, 8, 128).transpose(2, 0, 1))
    sh["npost_r"] = f(f(inp["norm_post"])[0].reshape(3, 8, 128).transpose(2, 0, 1))
    for nm, k1, k2 in (("1", "ffn1_w_gu", "ffn1_w_down"), ("2", "ffn2_w_gu", "ffn2_w_down")):
        wgu = f(inp[k1])[0]
        g = wgu[:, :FF].reshape(8, 128, FC, 128)
        u = wgu[:, FF:].reshape(8, 128, FC, 128)
        gu = np.concatenate([g, u], axis=3)
        sh["wgu%s_r" % nm] = f(gu.transpose(2, 1, 0, 3))
        wd = f(inp[k2])[0]
        sh["wdn%s_r" % nm] = f(wd.reshape(FC, 128, 8, 128).transpose(2, 1, 0, 3))
    win = f(inp["w_in"])[0]
    sh["win_r"] = f(win.reshape(8, 128, 44, 128).transpose(2, 1, 0, 3))
    sh["watt_r"] = f(f(inp["w_att_o"])[0].reshape(4, 128, 8, 128).transpose(2, 1, 0, 3))
    sh["wrec_r"] = f(f(inp["w_rec_o"])[0].reshape(8, 128, 8, 128).transpose(2, 1, 0, 3))
    sh["wout_r"] = f(f(inp["w_out"])[0].reshape(8, 128, 8, 128).transpose(2, 1, 0, 3))
    sh["convw_r"] = f(f(inp["conv_w"])[0].reshape(4, 8, 128).transpose(2, 1, 0))
    vec = lambda a: f(f(a)[0].reshape(8, 128).T)
    sh["convb_r"] = vec(inp["conv_b"])
    sh["lba_r"] = vec(inp["lru_ba"])
    sh["lbx_r"] = vec(inp["lru_bx"])
    sh["lam_r"] = vec(inp["lru_lambda"])
    for nm, k in (("lwa_r", "lru_wa"), ("lwx_r", "lru_wx")):
        w = f(inp[k])[0]
        sh[nm] = f(w.reshape(8, 2, 64, 64).transpose(1, 2, 0, 3).reshape(128, 8, 64))
    rb = f(inp["rel_bias"])[0]
    kk = np.arange(128)[:, None]
    qq = np.arange(128)[None, :]
    bg = np.empty((128, 8, 384), np.float32)
    for si, r in enumerate((0, 3, 4)):
        dist = qq - kk + 128 * (4 - r)
        rel = np.clip(dist, -128, 128) + 128
        tile = rb[:, rel]
        if r == 0:
            m = (qq >= 64) & (kk < 64)
        elif r == 4:
            m = (qq < 64) & (kk >= 64)
        else:
            m = np.zeros((128, 128), bool)
        tile = np.where(m[None], np.float32(NEG), tile)
        bg[:, :, si * 128:(si + 1) * 128] = tile.transpose(1, 0, 2)
    sh["biasg_r"] = f(bg)
    sh["biasc_r"] = f(np.broadcast_to(rb[:, 256][None, :], (128, 8)))
    sh["ident"] = np.eye(128, dtype=np.float32)
    return sh


def kernel(**inputs):
    x = np.asarray(inputs["x"], dtype=np.float32)
    c = np.asarray(inputs["c"], dtype=np.float32)
    if "nc" not in _NC_CACHE:
        _NC_CACHE["nc"] = build_program()
    nc = _NC_CACHE["nc"]
    sh = _prep_shared(inputs)
    in_maps = []
    for i in range(NCORES):
        b, half = i // 2, i % 2
        m = dict(sh)
        m["w_ada_r"] = m.pop("_w_ada_halves")[half]
        m["x_in"] = np.ascontiguousarray(x[b, half * T:(half + 1) * T, :])
        m["c_pc"] = np.ascontiguousarray(c[b].reshape(8, 128).T)
        m["flag"] = np.full((128, 1), float(half), np.float32)
        in_maps.append(m)
    res = run_bass_kernel_spmd(nc, in_maps, core_ids=list(range(NCORES)))
    out = np.empty((4, 4096, D), np.float32)
    for i in range(NCORES):
        b, half = i // 2, i % 2
        out[b, half * T:(half + 1) * T, :] = res.results[i]["out"]
    return out
```

```python
import numpy as np
import concourse.bass as bass
import concourse.mybir as mybir
from concourse.bass_utils import run_bass_kernel_spmd

F32 = mybir.dt.float32
BF16 = mybir.dt.bfloat16
AF = mybir.ActivationFunctionType
ALU = mybir.AluOpType

NCORES = 8
D = 1024
DC = 8
T = 2048
TT = 512
NT = 4
FF = 2816
FC = 22
EPS = 1e-6
NEG = -30000.0

SB_BASE = 16512
SB_TOP = 229344


class Emitter:
    ENGS = ("pe", "act", "dve", "pool", "sp")

    def __init__(self, nc):
        self.nc = nc
        self.streams = {e: [] for e in self.ENGS}
        self.cnt = {e: 0 for e in self.ENGS}
        self.known = {e: {} for e in self.ENGS}
        self.last_write = {}
        self.readers = {}
        self.dma_cnt = {}
        self.sem_names = set(self.ENGS)

    def _need(self, eng, ticket):
        if ticket is None:
            return
        src, val = ticket
        if src == eng and eng in ("pe", "sp"):
            return
        if self.known[eng].get(src, 0) >= val:
            return
        self.known[eng][src] = val
        self.streams[eng].append(("wait", src, val))

    def _deps(self, eng, reads, writes):
        for r in reads:
            self._need(eng, self.last_write.get(r))
        for w in writes:
            self._need(eng, self.last_write.get(w))
            for t in self.readers.get(w, ()):
                self._need(eng, t)

    def _commit(self, ticket, reads, writes):
        for r in reads:
            self.readers.setdefault(r, []).append(ticket)
        for w in writes:
            self.last_write[w] = ticket
            self.readers[w] = []

    def op(self, eng, fn, reads=(), writes=(), signal=True):
        self._deps(eng, reads, writes)
        if signal:
            self.cnt[eng] += 1
            ticket = (eng, self.cnt[eng])
        else:
            ticket = (eng, self.cnt[eng] + 1)
        self.streams[eng].append(("inst", fn, signal))
        self._commit(ticket, reads, writes)
        return ticket

    def dma(self, q, sem, fn, reads=(), writes=()):
        self.sem_names.add(sem)
        self._deps(q, reads, writes)
        self.dma_cnt[sem] = self.dma_cnt.get(sem, 0) + 16
        ticket = (sem, self.dma_cnt[sem])
        self.streams[q].append(("dma", fn, sem, 16))
        self._commit(ticket, reads, writes)
        return ticket

    def cc(self, sem, fn, reads=(), writes=()):
        self.sem_names.add(sem)
        self._deps("pool", reads, writes)
        self.dma_cnt[sem] = self.dma_cnt.get(sem, 0) + 1
        ticket = (sem, self.dma_cnt[sem])
        self.streams["pool"].append(("dma", fn, sem, 1))
        self._commit(ticket, reads, writes)
        return ticket

    def retarget(self, keys, sem):
        t = (sem, self.dma_cnt[sem])
        for k in keys:
            self.last_write[k] = t

    def barrier(self):
        for e in self.ENGS:
            for s in ("pe", "act", "dve", "pool"):
                if s != e and self.cnt[s] > 0:
                    self._need(e, (s, self.cnt[s]))
            if e in ("pe", "act", "dve", "pool") and self.cnt[e] > 0 and e != "pe":
                self._need(e, (e, self.cnt[e]))
            for s, v in self.dma_cnt.items():
                self._need(e, (s, v))

    def final_wait(self, eng, sems):
        for s in sems:
            self._need(eng, (s, self.dma_cnt[s]))

    def replay(self, eng, handle, sems):
        for item in self.streams[eng]:
            if item[0] == "wait":
                handle.wait_ge(sems[item[1]], item[2])
            elif item[0] == "inst":
                ins = item[1](handle)
                if item[2]:
                    ins.then_inc(sems[eng], 1)
            else:
                ins = item[1](handle)
                ins.then_inc(sems[item[2]], item[3])


def build_program():
    nc = bass.Bass("TRN2", target_bir_lowering=False)
    E = Emitter(nc)

    def din(name, shape, dt=F32):
        return nc.dram_tensor(name, list(shape), dt, kind="ExternalInput").ap()

    x_in = din("x_in", [T, D])
    out_d = nc.dram_tensor("out", [T, D], F32, kind="ExternalOutput").ap()
    c_pc = din("c_pc", [128, DC])
    w_ada = din("w_ada_r", [128, 36, 1024])
    b_ada = din("b_ada_r", [128, 72])
    npre = din("npre_r", [128, 3, DC])
    npost = din("npost_r", [128, 3, DC])
    wgu_d = [din("wgu1_r", [FC, 128, 8, 256]), din("wgu2_r", [FC, 128, 8, 256])]
    wdn_d = [din("wdn1_r", [DC, 128, FC, 128]), din("wdn2_r", [DC, 128, FC, 128])]
    win_d = din("win_r", [44, 128, 8, 128])
    watt_d = din("watt_r", [DC, 128, 4, 128])
    wrec_d = din("wrec_r", [DC, 128, 8, 128])
    wout_d = din("wout_r", [DC, 128, 8, 128])
    convw_d = din("convw_r", [128, DC, 4])
    convb_d = din("convb_r", [128, DC])
    lba_d = din("lba_r", [128, DC])
    lbx_d = din("lbx_r", [128, DC])
    lam_d = din("lam_r", [128, DC])
    lwa_d = din("lwa_r", [128, DC, 64])
    lwx_d = din("lwx_r", [128, DC, 64])
    biasg_d = din("biasg_r", [128, 8, 384])
    biasc_d = din("biasc_r", [128, 8])
    ident_d = din("ident", [128, 128])
    flag_d = din("flag", [128, 1])

    ccA_in = nc.dram_tensor("ccA_in", [128, 32], F32, kind="Internal").ap()
    ccA_out = nc.dram_tensor("ccA_out", [256, 32], F32, kind="Internal").ap()
    ccB_in = nc.dram_tensor("ccB_in", [128, 8], F32, kind="Internal").ap()
    ccB_out = nc.dram_tensor("ccB_out", [256, 8], F32, kind="Internal").ap()
    ccC_in = nc.dram_tensor("ccC_in", [128, 4096], BF16, kind="Internal").ap()
    ccC_out = nc.dram_tensor("ccC_out", [256, 4096], BF16, kind="Internal").ap()
    ccD_in = nc.dram_tensor("ccD_in", [128, 36], F32, kind="Internal").ap()
    ccD_out = nc.dram_tensor("ccD_out", [256, 36], F32, kind="Internal").ap()
    PAIRS = [[0, 1], [2, 3], [4, 5], [6, 7]]

    def sb(name, shape, dt, off):
        nbytes = int(np.prod(shape[1:])) * (4 if dt == F32 else 2)
        assert off % 32 == 0 and off >= SB_BASE and off + nbytes <= SB_TOP, (name, off, nbytes)
        return nc.alloc_sbuf_tensor_at(name, list(shape), dt, offset=off)

    o = SB_BASE
    xT = sb("xT", [128, DC, T], F32, o); o += DC * T * 4
    CB = o
    def cst(name, shape, dt):
        nonlocal o
        t = sb(name, shape, dt, o)
        nb = int(np.prod(shape[1:])) * (4 if dt == F32 else 2)
        o += (nb + 31) // 32 * 32
        return t
    ident = cst("ident", [128, 128], F32)
    ones_bf = cst("ones_bf", [128, 128], BF16)
    flagones = cst("flagones", [128, 64], BF16)
    zeros5 = cst("zeros5", [128, TT], BF16)
    mod = cst("mod", [128, 72], F32)
    badas = cst("badas", [128, 72], F32)
    npre_s = cst("npre_s", [128, 3, DC], F32)
    npost_s = cst("npost_s", [128, 3, DC], F32)
    gs32 = cst("gs32", [128, 3, DC], F32)
    gg32 = cst("gg32", [128, 3, DC], F32)
    cpc_s = cst("cpc_s", [128, DC], F32)
    cact = cst("cact", [128, DC], F32)
    cact_bf = cst("cact_bf", [128, DC], BF16)
    convw_s = cst("convw_s", [128, DC, 4], F32)
    convb_s = cst("convb_s", [128, DC], F32)
    lba_s = cst("lba_s", [128, DC], F32)
    lbx_s = cst("lbx_s", [128, DC], F32)
    lam_s = cst("lam_s", [128, DC], F32)
    nsp = cst("nsp", [128, DC], F32)
    hnsp = cst("hnsp", [128, DC], F32)
    hba = cst("hba", [128, DC], F32)
    hbx = cst("hbx", [128, DC], F32)
    qrt = cst("qrt", [128, 1], F32)
    flag_s = cst("flag_s", [128, 1], F32)
    epsb = cst("epsb", [128, 1], F32)
    biasc_s = cst("biasc_s", [128, 8], F32)
    hend = cst("hend", [128, DC], F32)
    hin = cst("hin", [128, DC], F32)
    hin_raw = cst("hin_raw", [128, DC], F32)
    xrt = cst("xrt", [128, DC, 4], F32)
    xrh_raw = cst("xrh_raw", [128, DC, 4], F32)
    xrh = cst("xrh", [128, DC, 4], F32)
    PB = (o + 31) // 32 * 32
    PSZ = SB_TOP - PB

    psum = nc.alloc_psum_tensor("psum", [128, 4096], F32)

    def bank(b, w=512, off=0):
        return psum[:, b * 512 + off: b * 512 + off + w]

    sems = {}

    consts = []

    def cload(dst, src, key):
        E.dma("sp", "const", lambda e, d=dst, s=src: e.dma_start(out=d, in_=s), writes=[key])
        consts.append(key)

    cload(ident[:], ident_d[:, :], "ident")
    cload(badas[:], b_ada[:, :], "badas")
    cload(npre_s[:], npre[:, :, :], "npre")
    cload(npost_s[:], npost[:, :, :], "npost")
    cload(cpc_s[:], c_pc[:, :], "cpc")
    cload(convw_s[:], convw_d[:, :, :], "convw")
    cload(convb_s[:], convb_d[:, :], "convb")
    cload(lba_s[:], lba_d[:, :], "lba")
    cload(lbx_s[:], lbx_d[:, :], "lbx")
    cload(lam_s[:], lam_d[:, :], "lam")
    cload(flag_s[:], flag_d[:, :], "flag")
    cload(biasc_s[:], biasc_d[:, :], "biasc")
    E.retarget(consts, "const")

    E.op("dve", lambda e: e.memset(ones_bf[:], 1.0), writes=["ones"])
    E.op("dve", lambda e: e.memset(zeros5[:], 0.0), writes=["zeros5"])
    E.op("dve", lambda e: e.memset(epsb[:], float(D * EPS)), writes=["epsb"])
    E.op("dve", lambda e: e.tensor_scalar(out=flagones[:], in0=ones_bf[:, 0:64], scalar1=flag_s[:, 0:1],
                                           scalar2=None, op0=ALU.mult), reads=["ones", "flag"], writes=["flagones"])
    E.op("act", lambda e: e.activation(out=cact[:], in_=cpc_s[:], func=AF.Silu), reads=["cpc"], writes=["cact"])
    E.op("act", lambda e: e.copy(out=cact_bf[:], in_=cact[:]), reads=["cact"], writes=["cactbf"])
    E.op("act", lambda e: e.activation(out=nsp[:], in_=lam_s[:], func=AF.Exp, scale=-1.0), reads=["lam"], writes=["nsp"])
    E.op("act", lambda e: e.activation(out=nsp[:], in_=nsp[:], func=AF.Ln, bias=1.0), reads=["nsp"], writes=["nsp"])
    E.op("dve", lambda e: e.tensor_scalar(out=hnsp[:], in0=nsp[:], scalar1=-4.0, scalar2=None, op0=ALU.mult),
         reads=["nsp"], writes=["hnsp"])
    E.op("dve", lambda e: e.tensor_scalar(out=nsp[:], in0=nsp[:], scalar1=-8.0, scalar2=None, op0=ALU.mult),
         reads=["nsp"], writes=["nsp"])
    E.op("dve", lambda e: e.tensor_scalar(out=hba[:], in0=lba_s[:], scalar1=0.5, scalar2=None, op0=ALU.mult), reads=["lba"], writes=["hba"])
    E.op("dve", lambda e: e.tensor_scalar(out=hbx[:], in0=lbx_s[:], scalar1=0.5, scalar2=None, op0=ALU.mult), reads=["lbx"], writes=["hbx"])
    E.op("dve", lambda e: e.memset(qrt[:], 0.25), writes=["qrt"])

    adab = [sb("adab%d" % i, [128, 9, 1024], BF16, PB + 16384 + i * 18432) for i in range(2)]
    modp = sb("modp", [128, 36], F32, PB + 16384 + 2 * 18432)
    modall = sb("modall", [128, 2, 36], F32, PB + 16384 + 2 * 18432 + 160)
    for pc in range(4):
        s = pc % 2
        for jj in range(9):
            E.dma("pool", "ada%d" % s, lambda e, s=s, pc=pc, jj=jj: e.dma_start(out=adab[s][:, jj, :], in_=w_ada[:, pc * 9 + jj, :]),
                  writes=[("adab", s, jj)])
        E.retarget([("adab", s, jj) for jj in range(9)], "ada%d" % s)
        for jj in range(9):
            j = pc * 9 + jj
            for kc in range(8):
                E.op("pe", lambda e, s=s, jj=jj, kc=kc, j=j: e.matmul(
                    bank(2, 1, j), lhsT=adab[s][:, jj, kc * 128:(kc + 1) * 128], rhs=cact_bf[:, kc:kc + 1],
                    start=(kc == 0), stop=(kc == 7)),
                    reads=[("adab", s, jj), "cactbf"], writes=[("ps", 2)], signal=(kc == 7 and jj == 8))
    E.op("dve", lambda e: e.tensor_copy(out=modp[:], in_=bank(2, 36)), reads=[("ps", 2)], writes=["modp"])
    E.dma("pool", "cioD1", lambda e: e.dma_start(out=ccD_in[:, :], in_=modp[:]), reads=["modp"], writes=["ccD_in"])
    E.cc("cc", lambda e: e.collective_compute("AllGather", ALU.bypass, replica_groups=PAIRS, ins=[ccD_in[:, :]], outs=[ccD_out[:, :]]),
         reads=["ccD_in"], writes=["ccD_out"])
    xin = [sb("xin%d" % i, [128, D], F32, PB + i * 4096) for i in range(4)]
    for tk in range(16):
        s = tk % 4
        E.dma("sp", "xin%d" % s, lambda e, s=s, tk=tk: e.dma_start(out=xin[s][:], in_=x_in[tk * 128:(tk + 1) * 128, :]),
              reads=([("adab", 1, 8)] if tk >= 8 else []), writes=[("xin", s)])
        for hb in range(2):
            bk = (tk * 2 + hb) % 2
            for cc in range(4):
                c = hb * 4 + cc
                E.op("pe", lambda e, s=s, c=c, bk=bk, cc=cc: e.transpose(
                    out=bank(bk, 128, cc * 128), in_=xin[s][:, c * 128:(c + 1) * 128], identity=ident[:]),
                    reads=[("xin", s), "ident"], writes=[("ps", bk)], signal=(cc == 3))
            eng = "act"
            if eng == "act":
                E.op("act", lambda e, hb=hb, tk=tk, bk=bk: e.copy(
                    out=xT[:, hb * 4:(hb + 1) * 4, tk * 128:(tk + 1) * 128],
                    in_=bank(bk).rearrange("p (c n) -> p c n", c=4)),
                    reads=[("ps", bk)], writes=[("x", tk // 4)])
            else:
                E.op("dve", lambda e, hb=hb, tk=tk, bk=bk: e.tensor_copy(
                    out=xT[:, hb * 4:(hb + 1) * 4, tk * 128:(tk + 1) * 128],
                    in_=bank(bk).rearrange("p (c n) -> p c n", c=4)),
                    reads=[("ps", bk)], writes=[("x", tk // 4)])

    E.dma("pool", "cioD2", lambda e: e.dma_start(out=modall[:], in_=ccD_out.rearrange("(r p) n -> p r n", p=128)),
          reads=["ccD_out"], writes=["modall"])
    E.op("dve", lambda e: e.tensor_tensor(out=mod[:], in0=modall[:].rearrange("p r n -> p (r n)"), in1=badas[:], op=ALU.add),
         reads=["modall", "badas"], writes=[("mod", 0), ("mod", 1), ("mod", 2)])
    modv = mod[:].rearrange("p (s k c) -> p s k c", s=3, k=3)
    for s_ in range(3):
        E.op("dve", lambda e, s_=s_: e.scalar_tensor_tensor(
            out=gs32[:, s_, :], in0=modv[:, s_, 1, :], scalar=1.0, in1=npre_s[:, s_, :], op0=ALU.add, op1=ALU.mult),
            reads=[("mod", s_), "npre"], writes=[("gs32", s_)])
        E.op("dve", lambda e, s_=s_: e.tensor_scalar(out=gs32[:, s_, :], in0=gs32[:, s_, :], scalar1=32.0,
                                                      scalar2=None, op0=ALU.mult), reads=[("gs32", s_)], writes=[("gs32", s_)])
        rw = 32.0 * (1.0 if s_ == 1 else 0.5)
        E.op("dve", lambda e, s_=s_, rw=rw: e.scalar_tensor_tensor(
            out=gg32[:, s_, :], in0=modv[:, s_, 2, :], scalar=rw, in1=npost_s[:, s_, :], op0=ALU.mult, op1=ALU.mult),
            reads=[("mod", s_), "npost"], writes=[("gg32", s_)])

    def shiftv(s_, c):
        return modv[:, s_, 0, c:c + 1]

    E.barrier()

    def prenorm(s_, tiles, hT, hcol0, tmp, pos=None):
        xsq, t1, rstd = tmp
        for i0, t in enumerate(tiles):
            i = pos[i0] if pos is not None else i0
            sbk = 6 + (i % 2)
            for c in range(DC):
                q = c % 2
                if c % 2 == 0:
                    E.op("act", lambda e, c=c, t=t, q=q: e.activation(out=xsq[q][:], in_=xT[:, c, t * TT:(t + 1) * TT], func=AF.Square),
                         reads=[("x", t)], writes=[("xsq", q)])
                else:
                    E.op("dve", lambda e, c=c, t=t, q=q: e.tensor_tensor(out=xsq[q][:], in0=xT[:, c, t * TT:(t + 1) * TT],
                                                                           in1=xT[:, c, t * TT:(t + 1) * TT], op=ALU.mult),
                         reads=[("x", t)], writes=[("xsq", q)])
                E.op("pe", lambda e, c=c, q=q, sbk=sbk: e.matmul(bank(sbk), lhsT=ones_bf[:], rhs=xsq[q][:], start=(c == 0), stop=(c == 7)),
                     reads=[("xsq", q), "ones"], writes=[("ps", sbk)], signal=True)
            r = i % 2
            E.op("act", lambda e, r=r, sbk=sbk: e.activation(out=rstd[r][:], in_=bank(sbk), func=AF.Ln, bias=epsb[:, 0:1]),
                 reads=[("ps", sbk), "epsb"], writes=[("rstd", r)])
            E.op("act", lambda e, r=r: e.activation(out=rstd[r][:], in_=rstd[r][:], func=AF.Exp, scale=-0.5),
                 reads=[("rstd", r)], writes=[("rstd", r)])
            for c in range(DC):
                q = c % 2
                E.op("dve", lambda e, c=c, t=t, q=q, r=r: e.tensor_tensor(out=t1[q][:], in0=xT[:, c, t * TT:(t + 1) * TT], in1=rstd[r][:], op=ALU.mult),
                     reads=[("x", t), ("rstd", r)], writes=[("t1", q)])
                if c in (3, 7):
                    E.op("dve", lambda e, c=c, i=i, q=q: e.tensor_scalar(out=hT[:, c, hcol0 + i * TT: hcol0 + (i + 1) * TT], in0=t1[q][:],
                                                                          scalar1=gs32[:, s_, c:c + 1], scalar2=shiftv(s_, c),
                                                                          op0=ALU.mult, op1=ALU.add),
                         reads=[("t1", q), ("gs32", s_), ("mod", s_)], writes=[("h", i, c)])
                else:
                    E.op("act", lambda e, c=c, i=i, q=q: e.activation(out=hT[:, c, hcol0 + i * TT: hcol0 + (i + 1) * TT], in_=t1[q][:],
                                                                       func=AF.Identity, bias=shiftv(s_, c), scale=gs32[:, s_, c:c + 1]),
                         reads=[("t1", q), ("gs32", s_), ("mod", s_)], writes=[("h", i, c)])

    def postnorm_residual(s_, t, ytile, ysq_ready_bank, tmp, extra=None, steps=None):
        t1, rstd = tmp
        r = t % 2

        def st_rstd():
            E.op("act", lambda e, r=r, b=ysq_ready_bank: e.activation(out=rstd[r][:], in_=bank(b), func=AF.Ln, bias=epsb[:, 0:1]),
                 reads=[("ps", ysq_ready_bank), "epsb"], writes=[("rstd", r)])
            E.op("act", lambda e, r=r: e.activation(out=rstd[r][:], in_=rstd[r][:], func=AF.Exp, scale=-0.5),
                 reads=[("rstd", r)], writes=[("rstd", r)])

        def st_chunk(c):
            q = c % 2
            E.op("dve", lambda e, c=c, q=q, r=r: e.tensor_tensor(out=t1[q][:], in0=ytile(c), in1=rstd[r][:], op=ALU.mult),
                 reads=[("y", t, c), ("rstd", r)] + (extra(c) if extra else []), writes=[("t1", q)])
            E.op("dve", lambda e, c=c, q=q, t=t: e.scalar_tensor_tensor(
                out=xT[:, c, t * TT:(t + 1) * TT], in0=t1[q][:], scalar=gg32[:, s_, c:c + 1], in1=xT[:, c, t * TT:(t + 1) * TT],
                op0=ALU.mult, op1=ALU.add),
                reads=[("t1", q), ("gg32", s_), ("x", t)], writes=[("x", t)])

        todo = [st_rstd] + [(lambda c=c: st_chunk(c)) for c in range(DC)]
        if steps is None:
            for f_ in todo:
                f_()
        else:
            steps.extend(todo)

    out_state = {"bufs": None, "n": 0}

    def out_step(tk, hb, copy_eng="act", bank_base=4):
        k = out_state["n"] % 2; out_state["n"] += 1
        buf = out_state["bufs"][k]
        bk = bank_base + k
        for cc in range(4):
            c = hb * 4 + cc
            E.op("pe", lambda e, c=c, cc=cc: e.transpose(
                out=bank(bk, 128, cc * 128), in_=xT[:, c, tk * 128:(tk + 1) * 128], identity=ident[:]),
                reads=[("x", tk // 4), "ident"], writes=[("ps", bk)], signal=(cc == 3))
        if copy_eng == "act":
            E.op("act", lambda e: e.copy(out=buf[:], in_=bank(bk)), reads=[("ps", bk)], writes=[("ost", k)])
        else:
            E.op("dve", lambda e: e.tensor_copy(out=buf[:], in_=bank(bk)), reads=[("ps", bk)], writes=[("ost", k)])
        E.dma("sp", "ost%d" % k, lambda e: e.dma_start(out=out_d[tk * 128:(tk + 1) * 128, hb * 512:(hb + 1) * 512], in_=buf[:]),
              reads=[("ost", k)], writes=[("outd", tk, hb)])

    def ffn(s_, wgu_src, wdn_src):
        o2 = PB
        hT = sb("f_hT%d" % s_, [128, DC, 1024], BF16, o2); o2 += 16384
        aT = sb("f_aT%d" % s_, [128, FC, 1024], BF16, o2); o2 += 45056
        y = sb("f_y%d" % s_, [128, DC, 1024], F32, o2); o2 += 32768
        wgu = []
        for i in range(3):
            wgu.append(sb("f_wgu%d_%d" % (s_, i), [128, 8, 256], BF16, o2)); o2 += 4096
        wdn = []
        for i in range(2):
            wdn.append(sb("f_wdn%d_%d" % (s_, i), [128, FC, 128], BF16, o2)); o2 += 5632
        xsq = []; t1 = []; sg = []; ysq = []; rstd = []
        for i in range(2):
            xsq.append(sb("f_xsq%d_%d" % (s_, i), [128, TT], BF16, o2)); o2 += 1024
            t1.append(sb("f_t1%d_%d" % (s_, i), [128, TT], F32, o2)); o2 += 2048
            sg.append(sb("f_sg%d_%d" % (s_, i), [128, TT], F32, o2)); o2 += 2048
            ysq.append(sb("f_ysq%d_%d" % (s_, i), [128, TT], BF16, o2)); o2 += 1024
            rstd.append(sb("f_rstd%d_%d" % (s_, i), [128, TT], F32, o2)); o2 += 2048
        if s_ == 2:
            out_state["bufs"] = []
            for i in range(2):
                out_state["bufs"].append(sb("f_ost%d" % i, [128, 512], F32, o2)); o2 += 2048
        assert o2 <= SB_TOP, o2
        it = 0
        deferred = []
        prenorm(s_, [0, 1], hT, 0, (xsq, t1, rstd))
        for g in range(2):
            tiles = [2 * g, 2 * g + 1]
            for f in range(FC):
                ws = f % 3
                E.dma("pool", "wgu%d" % ws, lambda e, ws=ws, f=f: e.dma_start(out=wgu[ws][:], in_=wgu_src[f]),
                      writes=[("wgu", ws)])
                for i in range(2):
                    pb_ = it % 2; it += 1
                    for kc in range(8):
                        E.op("pe", lambda e, ws=ws, kc=kc, i=i, pb_=pb_: e.matmul(
                            bank(pb_), lhsT=wgu[ws][:, kc, 0:128], rhs=hT[:, kc, i * TT:(i + 1) * TT], start=(kc == 0), stop=(kc == 7)),
                            reads=[("wgu", ws), ("h", i, kc)], writes=[("ps", pb_)], signal=(kc == 7))
                    for kc in range(8):
                        E.op("pe", lambda e, ws=ws, kc=kc, i=i, pb_=pb_: e.matmul(
                            bank(2 + pb_), lhsT=wgu[ws][:, kc, 128:256], rhs=hT[:, kc, i * TT:(i + 1) * TT], start=(kc == 0), stop=(kc == 7)),
                            reads=[("wgu", ws), ("h", i, kc)], writes=[("ps", 2 + pb_)], signal=(kc == 7))
                    E.op("act", lambda e, pb_=pb_: e.activation(out=sg[pb_][:], in_=bank(pb_), func=AF.Silu),
                         reads=[("ps", pb_)], writes=[("sg", pb_)])
                    E.op("dve", lambda e, pb_=pb_, f=f, i=i: e.tensor_tensor(
                        out=aT[:, f, i * TT:(i + 1) * TT], in0=sg[pb_][:], in1=bank(2 + pb_), op=ALU.mult),
                        reads=[("sg", pb_), ("ps", 2 + pb_)], writes=[("a", i, f)])
                for _ in range(2):
                    if deferred:
                        deferred.pop(0)()
            while deferred:
                deferred.pop(0)()
            if g == 0:
                prenorm(s_, [2, 3], hT, 0, (xsq, t1, rstd))
            def down_block(d, i, ws):
                nonlocal it
                t = tiles[i]
                pb_ = 4 + (it % 2); it += 1
                for fc in range(FC):
                    E.op("pe", lambda e, fc=fc: e.matmul(
                        bank(pb_), lhsT=wdn[ws][:, fc, :], rhs=aT[:, fc, i * TT:(i + 1) * TT], start=(fc == 0), stop=(fc == FC - 1)),
                        reads=[("wdn", ws, fc // 11), ("a", i, fc)], writes=[("ps", pb_)], signal=(fc == FC - 1))
                q = pb_ % 2
                E.op("act", lambda e: e.activation(out=ysq[q][:], in_=bank(pb_), func=AF.Square),
                     reads=[("ps", pb_)], writes=[("ysq", q)])
                E.op("act", lambda e: e.copy(out=y[:, d, i * TT:(i + 1) * TT], in_=bank(pb_)),
                     reads=[("ps", pb_)], writes=[("y", t, d), ("ybuf", i, d)])
                E.op("pe", lambda e: e.matmul(bank(6 + i), lhsT=ones_bf[:], rhs=ysq[q][:], start=(d == 0), stop=(d == 7)),
                     reads=[("ysq", q), "ones"], writes=[("ps", 6 + i)], signal=True)

            def load_wdn(d, ws):
                for hf in range(2):
                    E.dma("pool", "wdn%d" % ws, lambda e, hf=hf: e.dma_start(
                        out=wdn[ws][:, hf * 11:(hf + 1) * 11, :], in_=wdn_src[d, :, hf * 11:(hf + 1) * 11, :]),
                        writes=[("wdn", ws, hf)])
                E.retarget([("wdn", ws, 0), ("wdn", ws, 1)], "wdn%d" % ws)

            def pn(i, steps=None):
                postnorm_residual(s_, tiles[i], lambda c: y[:, c, i * TT:(i + 1) * TT], 6 + i, (t1, rstd),
                                  extra=lambda c: [("ybuf", i, c)], steps=steps)

            if g == 0:
                for d in range(DC):
                    ws = d % 2
                    load_wdn(d, ws)
                    for i in range(2):
                        down_block(d, i, ws)
                pn(0, deferred); pn(1, deferred)
                if s_ == 2:
                    for tk in range(8):
                        for hb in range(2):
                            deferred.append(lambda tk=tk, hb=hb: out_step(tk, hb, "act", 4))
            else:
                pend = []
                wn = 0
                for i in range(2):
                    for d in range(DC):
                        ws = wn % 2; wn += 1
                        load_wdn(d, ws)
                        down_block(d, i, ws)
                        for _ in range(4):
                            if pend:
                                pend.pop(0)()
                    if i == 0:
                        pn(0, pend)
                        if s_ == 2:
                            for tk in range(8, 12):
                                for hb in range(2):
                                    pend.append(lambda tk=tk, hb=hb: out_step(tk, hb, "act", 0))
                while pend:
                    pend.pop(0)()
                pn(1)
                if s_ == 2:
                    for tk in range(12, 16):
                        out_step(tk, 0, "act", 0); out_step(tk, 1, "dve", 0)
        E.barrier()

    def mixer():
        s_ = 1
        o2 = PB
        hT = sb("m_hT", [128, DC, T], BF16, o2); o2 += 32768
        R0 = sb("m_R0", [128, DC, T], BF16, o2); R0_OFF = o2; o2 += 32768
        R1 = sb("m_R1", [128, DC, T], BF16, o2); o2 += 32768
        QOFF = o2
        qT = sb("m_qT", [128, 4, T], BF16, o2); o2 += 16384
        VOFF = o2
        FREE_END = SB_TOP
        ot = VOFF
        xsq = []; t1 = []; rstd = []
        for i in range(2):
            xsq.append(sb("m_xsq%d" % i, [128, TT], BF16, ot)); ot += 1024
            t1.append(sb("m_t1%d" % i, [128, TT], F32, ot)); ot += 2048
            rstd.append(sb("m_rstd%d" % i, [128, TT], F32, ot)); ot += 2048
        prenorm(s_, [3], hT, 0, (xsq, t1, rstd), pos=[3])
        win = []
        for i in range(3):
            win.append(sb("m_win%d" % i, [128, 8, 128], BF16, QOFF + i * 2048))
        wcount = [0]

        def load_win(oc):
            ws = wcount[0] % 3; wcount[0] += 1
            wt = win[ws]
            E.dma("pool", "win%d" % ws, lambda e, wt=wt, oc=oc: e.dma_start(out=wt[:], in_=win_d[oc]), writes=[("win", ws)])
            return (wt, ("win", ws))

        for c in range(DC):
            ws = load_win(12 + c)
            for kc in range(8):
                E.op("pe", lambda e, ws=ws, kc=kc, c=c: e.matmul(bank(0, 3, c * 4), lhsT=ws[0][:, kc, :], rhs=hT[:, kc, T - 3:T],
                                                                  start=(kc == 0), stop=(kc == 7)),
                     reads=[ws[1], ("h", 3, kc)], writes=[("ps", 0)], signal=(kc == 7))
        E.op("dve", lambda e: e.memset(xrt[:].rearrange("p c j -> p (c j)"), 0.0), writes=["xrt"])
        E.op("dve", lambda e: e.tensor_copy(out=xrt[:, :, 0:3], in_=bank(0, 32).rearrange("p (c j) -> p c j", j=4)[:, :, 0:3]),
             reads=[("ps", 0)], writes=["xrt"])
        E.dma("pool", "cioA1", lambda e: e.dma_start(out=ccA_in[:, :], in_=xrt[:].rearrange("p c j -> p (c j)")),
              reads=["xrt"], writes=["ccA_in"])
        E.cc("cc", lambda e: e.collective_compute("AllGather", ALU.bypass, replica_groups=PAIRS, ins=[ccA_in[:, :]], outs=[ccA_out[:, :]]),
             reads=["ccA_in"], writes=["ccA_out"])
        ot = QOFF + 3 * 2048
        xrb = sb("m_xrb", [128, 4 + T], F32, ot); ot += (4 + T) * 4
        ot = (ot + 31) // 32 * 32
        names = ["U", "h0", "P"]
        L = {}
        for n_ in names:
            L[n_] = sb("m_L" + n_, [128, TT], F32, ot); ot += 2048
        xc2 = []; mu2 = []
        for i in range(2):
            xc2.append(sb("m_xc%d" % i, [128, TT], F32, ot)); ot += 2048
            mu2.append(sb("m_mu%d" % i, [128, TT], F32, ot)); ot += 2048
        xcb = sb("m_xcb", [128, TT], BF16, ot); ot += 1024
        gyb = sb("m_gyb", [128, T], F32, ot); ot += T * 4
        wblk = sb("m_wblk", [128, 2, DC, 128], BF16, ot); ot += 4096
        assert ot <= FREE_END, ot
        E.op("dve", lambda e: e.memset(wblk[:].rearrange("p a c n -> p (a c n)"), 0.0), writes=["wblk"])
        for a, src in enumerate((lwa_d, lwx_d)):
            for b in range(2):
                E.dma("pool", "wblk", lambda e, a=a, b=b, src=src: e.dma_start(
                    out=wblk[b * 64:(b + 1) * 64, a, :, b * 64:(b + 1) * 64], in_=src[b * 64:(b + 1) * 64, :, :]),
                    reads=["wblk"], writes=[("wblkd", a, b)])
        E.retarget([("wblkd", a, b) for a in range(2) for b in range(2)], "wblk")
        nxt0 = (load_win(12), load_win(20))
        prenorm(s_, [0, 1, 2], hT, 0, (xsq, t1, rstd), pos=[0, 1, 2])
        E.barrier()

        E.dma("pool", "cioA2", lambda e: e.dma_start(out=xrh_raw[:].rearrange("p c j -> p (c j)"), in_=ccA_out[0:128, :]),
              reads=["ccA_out"], writes=["xrh_raw"])
        E.op("dve", lambda e: e.tensor_scalar(out=xrh[:].rearrange("p c j -> p (c j)"), in0=xrh_raw[:].rearrange("p c j -> p (c j)"),
                                               scalar1=flag_s[:, 0:1], scalar2=None, op0=ALU.mult),
             reads=["xrh_raw", "flag"], writes=["xrh"])

        GB = [(2, 3), (6, 7)]

        def lru_G1(c, t, ws_y):
            pby = 4 + (t % 2)
            for kc in range(8):
                E.op("pe", lambda e, kc=kc: e.matmul(
                    bank(pby), lhsT=ws_y[0][:, kc, :], rhs=hT[:, kc, t * TT:(t + 1) * TT], start=(kc == 0), stop=(kc == 7)),
                    reads=[ws_y[1], ("h", t, kc)], writes=[("ps", pby)], signal=(kc == 7))
            E.op("act", lambda e: e.activation(out=gyb[:, t * TT:(t + 1) * TT], in_=bank(pby), func=AF.Gelu_apprx_tanh),
                 reads=[("ps", pby)], writes=[("gy", t)])

        def lru_F1(c, t, ws_x):
            pbk = t % 2
            for kc in range(8):
                E.op("pe", lambda e, kc=kc, pbk=pbk: e.matmul(
                    bank(pbk), lhsT=ws_x[0][:, kc, :], rhs=hT[:, kc, t * TT:(t + 1) * TT], start=(kc == 0), stop=(kc == 7)),
                    reads=[ws_x[1], ("h", t, kc)], writes=[("ps", pbk)], signal=(kc == 7))
            E.op("act", lambda e: e.copy(out=xrb[:, 4 + t * TT: 4 + (t + 1) * TT], in_=bank(pbk)),
                 reads=[("ps", pbk)], writes=[("xrb", t)])

        def lru_F2(c, t):
            ga, gi = GB[t % 2]
            xc_ = xc2[t % 2]; mu_ = mu2[t % 2]
            base = 4 + t * TT
            E.op("dve", lambda e: e.tensor_scalar(
                out=xc_[:], in0=xrb[:, base: base + TT], scalar1=convw_s[:, c, 3:4], scalar2=convb_s[:, c:c + 1],
                op0=ALU.mult, op1=ALU.add),
                reads=[("xrb", t), "convw", "convb"], writes=[("xc", t % 2)])
            for j in (2, 1, 0):
                E.op("dve", lambda e, j=j: e.scalar_tensor_tensor(
                    out=xc_[:], in0=xrb[:, base - 3 + j: base - 3 + j + TT], scalar=convw_s[:, c, j:j + 1], in1=xc_[:],
                    op0=ALU.mult, op1=ALU.add),
                    reads=[("xrb", t), ("xrb", t - 1), ("xc", t % 2)], writes=[("xc", t % 2)])
            E.op("act", lambda e: e.copy(out=xcb[:], in_=xc_[:]), reads=[("xc", t % 2)], writes=["xcb"])
            E.op("pe", lambda e: e.matmul(bank(ga), lhsT=wblk[:, 0, c, :], rhs=xcb[:], start=True, stop=True),
                 reads=["wblk", ("wblkd", 0, 0), ("wblkd", 0, 1), "xcb"], writes=[("ps", ga)], signal=True)
            E.op("pe", lambda e: e.matmul(bank(gi), lhsT=wblk[:, 1, c, :], rhs=xcb[:], start=True, stop=True),
                 reads=["wblk", ("wblkd", 1, 0), ("wblkd", 1, 1), "xcb"], writes=[("ps", gi)], signal=True)
            E.op("act", lambda e: e.activation(out=bank(ga), in_=bank(ga), func=AF.Tanh, bias=hba[:, c:c + 1], scale=0.5),
                 reads=["hba"], writes=[("ps", ga)])
            E.op("act", lambda e: e.activation(out=bank(gi), in_=bank(gi), func=AF.Tanh, bias=hbx[:, c:c + 1], scale=0.5),
                 reads=["hbx"], writes=[("ps", gi)])
            E.op("act", lambda e: e.activation(out=mu_[:], in_=bank(ga), func=AF.Exp, bias=nsp[:, c:c + 1], scale=nsp[:, c:c + 1]),
                 reads=[("ps", ga), "nsp"], writes=[("mu", t % 2)])
            E.op("act", lambda e: e.activation(out=bank(ga), in_=bank(ga), func=AF.Exp, bias=hnsp[:, c:c + 1], scale=hnsp[:, c:c + 1]),
                 reads=["hnsp"], writes=[("ps", ga)])
            E.op("act", lambda e: e.activation(out=mu_[:], in_=mu_[:], func=AF.Ln, bias=qrt[:, 0:1], scale=-0.25),
                 reads=["qrt"], writes=[("mu", t % 2)])
            E.op("act", lambda e: e.activation(out=mu_[:], in_=mu_[:], func=AF.Exp, scale=0.5),
                 reads=[], writes=[("mu", t % 2)])

        def lru_B(c, t):
            ga, gi = GB[t % 2]
            xc_ = xc2[t % 2]; mu_ = mu2[t % 2]
            E.op("dve", lambda e: e.scalar_tensor_tensor(out=L["U"][:], in0=bank(gi), scalar=1.0, in1=xc_[:], op0=ALU.add, op1=ALU.mult),
                 reads=[("ps", gi), ("xc", t % 2)], writes=["U"])
            E.op("dve", lambda e: e.tensor_tensor(out=L["U"][:], in0=L["U"][:], in1=mu_[:], op=ALU.mult),
                 reads=[("mu", t % 2)], writes=["U"])
            ini_h = 0.0 if t == 0 else hend[:, c:c + 1]
            ini_p = 1.0 if t == 0 else hin_raw[:, c:c + 1]
            E.op("dve", lambda e: e.tensor_tensor_scan(out=L["h0"][:], data0=bank(ga), data1=L["U"][:], initial=ini_h,
                                                       op0=ALU.mult, op1=ALU.add),
                 reads=[("ps", ga), "U", "hend"], writes=["h0"])
            E.op("dve", lambda e: e.tensor_tensor_scan(out=L["P"][:], data0=bank(ga), data1=zeros5[:], initial=ini_p,
                                                       op0=ALU.mult, op1=ALU.add),
                 reads=[("ps", ga), "zeros5", "pend"], writes=["P"])
            E.op("dve", lambda e: e.tensor_copy(out=hend[:, c:c + 1], in_=L["h0"][:, TT - 1:TT]), reads=["h0"], writes=["hend"])
            E.op("dve", lambda e: e.tensor_copy(out=hin_raw[:, c:c + 1], in_=L["P"][:, TT - 1:TT]), reads=["P"], writes=["pend"])
            E.op("pool", lambda e: e.tensor_tensor(out=R0[:, c, t * TT:(t + 1) * TT], in0=L["h0"][:], in1=gyb[:, t * TT:(t + 1) * TT], op=ALU.mult),
                 reads=["h0", ("gy", t)], writes=[("R0", c)])
            E.op("pool", lambda e: e.tensor_tensor(out=R1[:, c, t * TT:(t + 1) * TT], in0=L["P"][:], in1=gyb[:, t * TT:(t + 1) * TT], op=ALU.mult),
                 reads=["P", ("gy", t)], writes=[("R1", c)])

        nxt = nxt0
        for t in range(NT):
            lru_G1(0, t, nxt[1])
        for c in range(DC):
            ws_x, ws_y = nxt
            E.op("act", lambda e, c=c: e.copy(out=xrb[:, 1:4], in_=xrh[:, c, 0:3]), reads=["xrh"], writes=[("xrb", -1)])
            lru_F1(c, 0, ws_x)
            lru_F1(c, 1, ws_x)
            lru_F2(c, 0)
            lru_F1(c, 2, ws_x)
            lru_F2(c, 1)
            lru_B(c, 0)
            lru_F1(c, 3, ws_x)
            lru_F2(c, 2)
            lru_B(c, 1)
            lru_F2(c, 3)
            if c + 1 < DC:
                nxt = (load_win(12 + c + 1), load_win(20 + c + 1))
            lru_B(c, 2)
            if c + 1 < DC:
                lru_G1(c + 1, 0, nxt[1]); lru_G1(c + 1, 1, nxt[1]); lru_G1(c + 1, 2, nxt[1])
            lru_B(c, 3)
            if c + 1 < DC:
                lru_G1(c + 1, 3, nxt[1])
        E.barrier()
        E.dma("pool", "cioB1", lambda e: e.dma_start(out=ccB_in[:, :], in_=hend[:]), reads=["hend"], writes=["ccB_in"])
        E.cc("cc", lambda e: e.collective_compute("AllGather", ALU.bypass, replica_groups=PAIRS, ins=[ccB_in[:, :]], outs=[ccB_out[:, :]]),
             reads=["ccB_in"], writes=["ccB_out"])

        ot = VOFF
        win = []
        for i in range(3):
            win.append(sb("m2_win%d" % i, [128, 8, 128], BF16, ot)); ot += 2048
        wrc = []
        for i in range(2):
            wrc.append(sb("m2_wrc%d" % i, [128, 8, 128], BF16, ot)); ot += 2048
        sgt = []
        for i in range(2):
            sgt.append(sb("m2_sg%d" % i, [128, TT], F32, ot)); ot += 2048
        assert ot <= FREE_END
        it = 0
        for oc in range(4):
            ws = load_win(oc)
            for t in range(NT):
                pbk = it % 2; it += 1
                for kc in range(8):
                    E.op("pe", lambda e, ws=ws, kc=kc, t=t, pbk=pbk: e.matmul(
                        bank(pbk), lhsT=ws[0][:, kc, :], rhs=hT[:, kc, t * TT:(t + 1) * TT], start=(kc == 0), stop=(kc == 7)),
                        reads=[ws[1], ("h", t, kc)], writes=[("ps", pbk)], signal=(kc == 7))
                E.op("act", lambda e, oc=oc, t=t, pbk=pbk: e.activation(out=qT[:, oc, t * TT:(t + 1) * TT], in_=bank(pbk), func=AF.Copy, scale=0.125),
                     reads=[("ps", pbk)], writes=[("q", oc)])
        E.dma("pool", "cioB2", lambda e: e.dma_start(out=hin_raw[:], in_=ccB_out[0:128, :]), reads=["ccB_out", "pend"], writes=["hin_raw"])
        E.op("dve", lambda e: e.tensor_scalar(out=hin[:], in0=hin_raw[:], scalar1=flag_s[:, 0:1], scalar2=None, op0=ALU.mult),
             reads=["hin_raw", "flag"], writes=["hin"])
        for c in range(DC):
            E.op("dve", lambda e, c=c: e.scalar_tensor_tensor(out=R0[:, c, :], in0=R1[:, c, :], scalar=hin[:, c:c + 1], in1=R0[:, c, :],
                                                             op0=ALU.mult, op1=ALU.add),
                 reads=[("R0", c), ("R1", c), "hin"], writes=[("R0", c)])
        Z = R1
        for oc in range(DC):
            wr = oc % 2
            E.dma("pool", "wrc%d" % wr, lambda e, wr=wr, oc=oc: e.dma_start(out=wrc[wr][:], in_=wrec_d[oc]), writes=[("wrc", wr)])
            ws = load_win(36 + oc)
            for t in range(NT):
                pbk = it % 2; it += 1
                for c in range(8):
                    E.op("pe", lambda e, wr=wr, c=c, t=t, pbk=pbk: e.matmul(
                        bank(pbk), lhsT=wrc[wr][:, c, :], rhs=R0[:, c, t * TT:(t + 1) * TT], start=(c == 0), stop=(c == 7)),
                        reads=[("wrc", wr), ("R0", c)], writes=[("ps", pbk)], signal=(c == 7))
                for kc in range(8):
                    E.op("pe", lambda e, ws=ws, kc=kc, t=t, pbk=pbk: e.matmul(
                        bank(2 + pbk), lhsT=ws[0][:, kc, :], rhs=hT[:, kc, t * TT:(t + 1) * TT], start=(kc == 0), stop=(kc == 7)),
                        reads=[ws[1], ("h", t, kc)], writes=[("ps", 2 + pbk)], signal=(kc == 7))
                E.op("act", lambda e, pbk=pbk: e.activation(out=sgt[pbk][:], in_=bank(2 + pbk), func=AF.Sigmoid),
                     reads=[("ps", 2 + pbk)], writes=[("sgt", pbk)])
                E.op("dve", lambda e, oc=oc, t=t, pbk=pbk: e.tensor_tensor(out=Z[:, oc, t * TT:(t + 1) * TT], in0=sgt[pbk][:], in1=bank(pbk), op=ALU.mult),
                     reads=[("sgt", pbk), ("ps", pbk)], writes=[("Z", oc, t), ("R1", oc)])
        E.barrier()

        kT = sb("m_kT", [128, 4, 2560], BF16, R0_OFF)
        biasg = sb("m_biasg", [128, 8, 384], F32, R0_OFF + 20480)
        v = sb("m_v", [128, 20, 512], BF16, VOFF)
        ot = VOFF + 20480
        win = [sb("m1_win%d" % i, [128, 8, 128], BF16, VOFF + 20480 + i * 2048) for i in range(3)]
        E.dma("sp", "biasg", lambda e: e.dma_start(out=biasg[:].rearrange("p h n -> p (h n)"), in_=biasg_d.rearrange("p h n -> p (h n)")),
              writes=["biasg"])
        for h in range(8):
            E.op("dve", lambda e, h=h: e.tensor_scalar(out=biasg[:, h, :], in0=biasg[:, h, :], scalar1=biasc_s[:, h:h + 1], scalar2=None,
                                                        op0=ALU.subtract), reads=["biasg", "biasc"], writes=["biasg"])
        for oc in range(4, 8):
            ws = load_win(oc)
            for t in range(NT):
                pbk = it % 2; it += 1
                for kc in range(8):
                    E.op("pe", lambda e, ws=ws, kc=kc, t=t, pbk=pbk: e.matmul(
                        bank(pbk), lhsT=ws[0][:, kc, :], rhs=hT[:, kc, t * TT:(t + 1) * TT], start=(kc == 0), stop=(kc == 7)),
                        reads=[ws[1], ("h", t, kc)], writes=[("ps", pbk)], signal=(kc == 7))
                E.op("act", lambda e, oc=oc, t=t, pbk=pbk: e.copy(out=kT[:, oc - 4, 512 + t * TT: 512 + (t + 1) * TT], in_=bank(pbk)),
                     reads=[("ps", pbk)], writes=[("k", oc - 4, 1 + t)])
        for hp in range(4):
            ws = load_win(8 + hp)
            for tq in range(4):
                pbk = it % 2; it += 1
                for tk4 in range(4):
                    tk = tq * 4 + tk4
                    for kc in range(8):
                        E.op("pe", lambda e, ws=ws, kc=kc, tk=tk, tk4=tk4, pbk=pbk: e.matmul(
                            bank(pbk, 128, tk4 * 128), lhsT=hT[:, kc, tk * 128:(tk + 1) * 128], rhs=ws[0][:, kc, :],
                            start=(kc == 0), stop=(kc == 7)),
                            reads=[ws[1], ("h", tk // 4, kc)], writes=[("ps", pbk)], signal=(kc == 7 and tk4 == 3))
                E.op("act", lambda e, hp=hp, tq=tq, pbk=pbk: e.copy(
                    out=v[:, 4 + tq * 4: 4 + tq * 4 + 4, hp * 128:(hp + 1) * 128], in_=bank(pbk).rearrange("p (a n) -> p a n", a=4)),
                    reads=[("ps", pbk)], writes=[("v", 1 + tq)])
        E.barrier()
        for oc in range(4):
            E.dma("pool", "cioC1", lambda e, oc=oc: e.dma_start(out=ccC_in[:, oc * 512:(oc + 1) * 512], in_=kT[:, oc, 2048:2560]),
                  reads=[("k", oc, 4)], writes=[("ccC_in", oc)])
        E.dma("pool", "cioC1", lambda e: e.dma_start(out=ccC_in[:, 2048:4096], in_=v[:, 16:20, :].rearrange("p a n -> p (a n)")),
              reads=[("v", 4)], writes=[("ccC_in", 4)])
        E.retarget([("ccC_in", i) for i in range(5)], "cioC1")
        E.cc("cc", lambda e: e.collective_compute("AllGather", ALU.bypass, replica_groups=PAIRS, ins=[ccC_in[:, :]], outs=[ccC_out[:, :]]),
             reads=[("ccC_in", i) for i in range(5)], writes=["ccC_out"])

        def load_halo():
            for oc in range(4):
                E.dma("pool", "cioC2", lambda e, oc=oc: e.dma_start(out=kT[:, oc, 0:512], in_=ccC_out[0:128, oc * 512:(oc + 1) * 512]),
                      reads=["ccC_out"], writes=[("k", oc, 0)])
            E.dma("pool", "cioC2", lambda e: e.dma_start(out=v[:, 0:4, :].rearrange("p a n -> p (a n)"), in_=ccC_out[0:128, 2048:4096]),
                  reads=["ccC_out"], writes=[("v", 0)])
            E.retarget([("k", oc, 0) for oc in range(4)] + [("v", 0)], "cioC2")
            E.op("dve", lambda e: e.tensor_scalar(out=v[:, 0:4, :].rearrange("p a n -> p (a n)"), in0=v[:, 0:4, :].rearrange("p a n -> p (a n)"),
                                                   scalar1=flag_s[:, 0:1], scalar2=None, op0=ALU.mult),
                 reads=[("v", 0), "flag"], writes=[("v", 0)])


        ET = []; rc = []
        for i in range(2):
            ET.append(sb("m4_ET%d" % i, [128, 2, 640], BF16, ot)); ot += 2560
        for i in range(2):
            rc.append(sb("m4_rc%d" % i, [128, 128], F32, ot)); ot += 512
        assert ot <= FREE_END, ot
        attT = qT
        ROLE_SLOT = {1: 0, 2: 1, 0: 2, 3: 3, 4: 4}
        items = [(j, hp) for j in list(range(4, 16)) + list(range(4)) for hp in range(4)]

        def stageA(idx):
            j, hp = items[idx]
            sl = idx % 2
            base = 1536 * sl
            pkeys = [("ps", 3 * sl), ("ps", 3 * sl + 1), ("ps", 3 * sl + 2)]
            for r in range(5):
                slot = ROLE_SLOT[r]
                kt = j + r
                for hh in range(2):
                    p0 = hh * 64
                    E.op("pe", lambda e, p0=p0, kt=kt, slot=slot, hh=hh: e.matmul(
                        psum[:, base + hh * 768 + slot * 128: base + hh * 768 + (slot + 1) * 128],
                        lhsT=kT[p0:p0 + 64, hp, kt * 128:(kt + 1) * 128], rhs=qT[p0:p0 + 64, hp, j * 128:(j + 1) * 128],
                        start=True, stop=True),
                        reads=[("k", hp, kt // 4), ("q", hp)], writes=pkeys, signal=(r == 4 and hh == 1))
            sview = psum[:, base: base + 1536].rearrange("p (h n) -> p h n", h=2)
            E.op("dve", lambda e: e.tensor_tensor(
                out=sview[:, :, 256:640], in0=sview[:, :, 256:640], in1=biasg[:, 2 * hp:2 * hp + 2, :], op=ALU.add),
                reads=["biasg"], writes=pkeys)
            E.op("act", lambda e: e.activation(out=ET[sl][:], in_=sview[:, :, 0:640], func=AF.Exp),
                 reads=pkeys, writes=[("ET", sl)])

        def stageB(idx):
            j, hp = items[idx]
            sl = idx % 2
            pvb = 6 + sl
            for r in range(5):
                slot = ROLE_SLOT[r]; kt = j + r
                for hh in range(2):
                    p0 = hh * 64; h = 2 * hp + hh
                    E.op("pe", lambda e, kt=kt, h=h, p0=p0, slot=slot, r=r, hh=hh: e.matmul(
                        psum[p0:p0 + 64, pvb * 512: pvb * 512 + 128], lhsT=v[:, kt, h * 64:(h + 1) * 64], rhs=ET[sl][:, hh, slot * 128:(slot + 1) * 128],
                        start=(r == 0), stop=(r == 4)),
                        reads=[("v", kt // 4), ("ET", sl)], writes=[("ps", pvb)], signal=False)
            for r in range(5):
                slot = ROLE_SLOT[r]; kt = j + r
                on = flagones if kt < 4 else ones_bf
                for hh in range(2):
                    p0 = hh * 64
                    E.op("pe", lambda e, p0=p0, slot=slot, r=r, hh=hh, on=on: e.matmul(
                        psum[p0:p0 + 64, pvb * 512 + 128: pvb * 512 + 256], lhsT=on[:, 0:64], rhs=ET[sl][:, hh, slot * 128:(slot + 1) * 128],
                        start=(r == 0), stop=(r == 4)),
                        reads=["ones", "flagones", ("ET", sl)], writes=[("ps", pvb)], signal=(r == 4 and hh == 1))
            E.op("act", lambda e: e.activation(out=rc[sl][:], in_=psum[:, pvb * 512 + 128: pvb * 512 + 256], func=AF.Ln),
                 reads=[("ps", pvb)], writes=[("rc", sl)])
            E.op("act", lambda e: e.activation(out=rc[sl][:], in_=rc[sl][:], func=AF.Exp, scale=-1.0),
                 reads=[], writes=[("rc", sl)])
            E.op("dve", lambda e: e.tensor_tensor(
                out=attT[:, hp, j * 128:(j + 1) * 128], in0=psum[:, pvb * 512: pvb * 512 + 128], in1=rc[sl][:], op=ALU.mult),
                reads=[("ps", pvb), ("rc", sl), ("q", hp)], writes=[("att", hp, j // 4)])

        n_it = len(items)
        first_halo = min(i for i, (j, hp) in enumerate(items) if j < 4)
        stageA(0)
        for idx in range(n_it):
            if idx + 1 < n_it:
                if idx + 1 == first_halo:
                    load_halo()
                stageA(idx + 1)
            stageB(idx)
        E.barrier()

        ot = R0_OFF
        wout = sb("m6_wout", [128, DC, DC, 128], BF16, ot); ot += 16384
        ytile = sb("m6_y", [128, DC, TT], F32, ot); ot += 16384
        ot = VOFF
        win = [None] * 3
        for i in range(3):
            win[i] = sb("m6_win%d" % i, [128, 8, 128], BF16, ot); ot += 2048
        wat = []
        for i in range(2):
            wat.append(sb("m6_wat%d" % i, [128, 4, 128], BF16, ot)); ot += 1024
        sga = []; m1 = []; ysq = []; t1b = []; rstdb = []
        for i in range(2):
            sga.append(sb("m6_sga%d" % i, [128, TT], F32, ot)); ot += 2048
            m1.append(sb("m6_m1%d" % i, [128, TT], F32, ot)); ot += 2048
            ysq.append(sb("m6_ysq%d" % i, [128, TT], BF16, ot)); ot += 1024
            t1b.append(sb("m6_t1%d" % i, [128, TT], F32, ot)); ot += 2048
            rstdb.append(sb("m6_rstd%d" % i, [128, TT], F32, ot)); ot += 2048
        assert ot <= FREE_END, ot
        merged = Z
        it = 0
        for oc in range(DC):
            wa = oc % 2
            E.dma("pool", "wat%d" % wa, lambda e, wa=wa, oc=oc: e.dma_start(out=wat[wa][:], in_=watt_d[oc]), writes=[("wat", wa)])
            ws = load_win(28 + oc)
            if oc == 0:
                for dcx in range(DC):
                    E.dma("pool", "wout", lambda e, dcx=dcx: e.dma_start(out=wout[:, dcx, :, :], in_=wout_d[dcx]), writes=[("wout", dcx)])
                E.retarget([("wout", dcx) for dcx in range(DC)], "wout")
            for t in range(NT):
                pbk = it % 2; it += 1
                for hc in range(4):
                    E.op("pe", lambda e, wa=wa, hc=hc, t=t, pbk=pbk: e.matmul(
                        bank(pbk), lhsT=wat[wa][:, hc, :], rhs=attT[:, hc, t * TT:(t + 1) * TT], start=(hc == 0), stop=(hc == 3)),
                        reads=[("wat", wa), ("att", hc, t)], writes=[("ps", pbk)], signal=(hc == 3))
                for kc in range(8):
                    E.op("pe", lambda e, ws=ws, kc=kc, t=t, pbk=pbk: e.matmul(
                        bank(2 + pbk), lhsT=ws[0][:, kc, :], rhs=hT[:, kc, t * TT:(t + 1) * TT], start=(kc == 0), stop=(kc == 7)),
                        reads=[ws[1], ("h", t, kc)], writes=[("ps", 2 + pbk)], signal=(kc == 7))
                E.op("act", lambda e, pbk=pbk: e.activation(out=sga[pbk][:], in_=bank(2 + pbk), func=AF.Sigmoid),
                     reads=[("ps", 2 + pbk)], writes=[("sga", pbk)])
                E.op("dve", lambda e, pbk=pbk: e.tensor_tensor(out=m1[pbk][:], in0=sga[pbk][:], in1=bank(pbk), op=ALU.mult),
                     reads=[("sga", pbk), ("ps", pbk)], writes=[("m1", pbk)])
                E.op("dve", lambda e, oc=oc, t=t, pbk=pbk: e.tensor_tensor(
                    out=merged[:, oc, t * TT:(t + 1) * TT], in0=m1[pbk][:], in1=Z[:, oc, t * TT:(t + 1) * TT], op=ALU.add),
                    reads=[("m1", pbk), ("Z", oc, t)], writes=[("mg", oc, t)])
        E.barrier()
        ytile2 = sb("m6_y2", [128, DC, TT], F32, PB)
        yts = [ytile, ytile2]
        for t in range(NT):
            yt_ = yts[t % 2]
            for dcx in range(DC):
                pbk = 4 + (it % 2); it += 1
                for c in range(8):
                    E.op("pe", lambda e, dcx=dcx, c=c, t=t, pbk=pbk: e.matmul(
                        bank(pbk), lhsT=wout[:, dcx, c, :], rhs=merged[:, c, t * TT:(t + 1) * TT], start=(c == 0), stop=(c == 7)),
                        reads=[("wout", dcx), ("mg", c, t)], writes=[("ps", pbk)], signal=(c == 7))
                q = pbk % 2
                E.op("act", lambda e, pbk=pbk, q=q: e.activation(out=ysq[q][:], in_=bank(pbk), func=AF.Square),
                     reads=[("ps", pbk)], writes=[("ysq", q)])
                E.op("act", lambda e, pbk=pbk, dcx=dcx, yt_=yt_: e.copy(out=yt_[:, dcx, :], in_=bank(pbk)),
                     reads=[("ps", pbk)], writes=[("y", t, dcx), ("ybuf", t % 2, dcx)])
                E.op("pe", lambda e, q=q, t=t, dcx=dcx: e.matmul(bank(6 + t % 2), lhsT=ones_bf[:], rhs=ysq[q][:], start=(dcx == 0), stop=(dcx == 7)),
                     reads=[("ysq", q), "ones"], writes=[("ps", 6 + t % 2)], signal=True)
            postnorm_residual(s_, t, lambda c, yt_=yt_: yt_[:, c, :], 6 + t % 2, (t1b, rstdb), extra=lambda c, t=t: [("ybuf", t % 2, c)])
        E.barrier()

    ffn(0, wgu_d[0], wdn_d[0])
    mixer()
    ffn(2, wgu_d[1], wdn_d[1])

    E.final_wait("sp", ["ost0", "ost1"])

    import contextlib
    with contextlib.ExitStack() as stack:
        for name in sorted(E.sem_names):
            sems[name] = stack.enter_context(nc.semaphore(name))
        block = stack.enter_context(nc.Block())

        @block.tensor
        def _(e):
            E.replay("pe", e, sems)

        @block.scalar
        def _(e):
            E.replay("act", e, sems)

        @block.vector
        def _(e):
            E.replay("dve", e, sems)

        @block.gpsimd
        def _(e):
            E.replay("pool", e, sems)

        @block.sync
        def _(e):
            E.replay("sp", e, sems)
    return nc


_NC_CACHE = {}


def _prep_shared(inp):
    f = lambda a: np.ascontiguousarray(np.asarray(a, dtype=np.float32))
    sh = {}
    wa = f(inp["w_ada"])[0]
    sh["_w_ada_halves"] = f(wa.reshape(8, 128, 2, 36, 128).transpose(2, 1, 3, 0, 4).reshape(2, 128, 36, 1024))
    sh["b_ada_r"] = f(f(inp["b_ada"])[0].reshape(72, 128).T)
    sh["npre_r"] = f(f(inp["norm_pre"])[0].reshape(3, 8, 128).transpose(2, 0, 1))
    sh["npost_r"] = f(f(inp["norm_post"])[0].reshape(3, 8, 128).transpose(2, 0, 1))
    for nm, k1, k2 in (("1", "ffn1_w_gu", "ffn1_w_down"), ("2", "ffn2_w_gu", "ffn2_w_down")):
        wgu = f(inp[k1])[0]
        g = wgu[:, :FF].reshape(8, 128, FC, 128)
        u = wgu[:, FF:].reshape(8, 128, FC, 128)
        gu = np.concatenate([g, u], axis=3)
        sh["wgu%s_r" % nm] = f(gu.transpose(2, 1, 0, 3))
        wd = f(inp[k2])[0]
        sh["wdn%s_r" % nm] = f(wd.reshape(FC, 128, 8, 128).transpose(2, 1, 0, 3))
    win = f(inp["w_in"])[0]
    sh["win_r"] = f(win.reshape(8, 128, 44, 128).transpose(2, 1, 0, 3))
    sh["watt_r"] = f(f(inp["w_att_o"])[0].reshape(4, 128, 8, 128).transpose(2, 1, 0, 3))
    sh["wrec_r"] = f(f(inp["w_rec_o"])[0].reshape(8, 128, 8, 128).transpose(2, 1, 0, 3))
    sh["wout_r"] = f(f(inp["w_out"])[0].reshape(8, 128, 8, 128).transpose(2, 1, 0, 3))
    sh["convw_r"] = f(f(inp["conv_w"])[0].reshape(4, 8, 128).transpose(2, 1, 0))
    vec = lambda a: f(f(a)[0].reshape(8, 128).T)
    sh["convb_r"] = vec(inp["conv_b"])
    sh["lba_r"] = vec(inp["lru_ba"])
    sh["lbx_r"] = vec(inp["lru_bx"])
    sh["lam_r"] = vec(inp["lru_lambda"])
    for nm, k in (("lwa_r", "lru_wa"), ("lwx_r", "lru_wx")):
        w = f(inp[k])[0]
        sh[nm] = f(w.reshape(8, 2, 64, 64).transpose(1, 2, 0, 3).reshape(128, 8, 64))
    rb = f(inp["rel_bias"])[0]
    kk = np.arange(128)[:, None]
    qq = np.arange(128)[None, :]
    bg = np.empty((128, 8, 384), np.float32)
    for si, r in enumerate((0, 3, 4)):
        dist = qq - kk + 128 * (4 - r)
        rel = np.clip(dist, -128, 128) + 128
        tile = rb[:, rel]
        if r == 0:
            m = (qq >= 64) & (kk < 64)
        elif r == 4:
            m = (qq < 64) & (kk >= 64)
        else:
            m = np.zeros((128, 128), bool)
        tile = np.where(m[None], np.float32(NEG), tile)
        bg[:, :, si * 128:(si + 1) * 128] = tile.transpose(1, 0, 2)
    sh["biasg_r"] = f(bg)
    sh["biasc_r"] = f(np.broadcast_to(rb[:, 256][None, :], (128, 8)))
    sh["ident"] = np.eye(128, dtype=np.float32)
    return sh


def kernel(**inputs):
    x = np.asarray(inputs["x"], dtype=np.float32)
    c = np.asarray(inputs["c"], dtype=np.float32)
    if "nc" not in _NC_CACHE:
        _NC_CACHE["nc"] = build_program()
    nc = _NC_CACHE["nc"]
    sh = _prep_shared(inputs)
    in_maps = []
    for i in range(NCORES):
        b, half = i // 2, i % 2
        m = dict(sh)
        m["w_ada_r"] = m.pop("_w_ada_halves")[half]
        m["x_in"] = np.ascontiguousarray(x[b, half * T:(half + 1) * T, :])
        m["c_pc"] = np.ascontiguousarray(c[b].reshape(8, 128).T)
        m["flag"] = np.full((128, 1), float(half), np.float32)
        in_maps.append(m)
    res = run_bass_kernel_spmd(nc, in_maps, core_ids=list(range(NCORES)))
    out = np.empty((4, 4096, D), np.float32)
    for i in range(NCORES):
        b, half = i // 2, i % 2
        out[b, half * T:(half + 1) * T, :] = res.results[i]["out"]
    return out
```
